# Optimizing a Trainium2 kernel written in Bass

```python
import math, functools
import jax, jax.numpy as jnp
from jax import lax
import numpy as np

D_MODEL = 1024
BATCH = 4
SEQ = 8192
DEPTH = 4

N_MIXERS = 3
MEM_LEN = 256
BLOCK = 128
ROPE_THETA = 10000.0
NEG = -1e30
LN_EPS = 1e-5
RMS_EPS = 1e-6

A_HEADS = 16
A_KV_HEADS = 4
A_HEAD_DIM = 64
A_WINDOW = 128

LRU_WIDTH = D_MODEL
LRU_BLOCKS = 4
LRU_BLOCK_W = LRU_WIDTH // LRU_BLOCKS
LRU_CONV = 4
LRU_C = 8.0

C_HEADS = 8
C_NOPE = 128
C_ROPE = 64
C_V = 128
C_Q_RANK = 384
C_KV_RANK = 256

X_HEADS = 4
X_HEAD_DIM = D_MODEL // X_HEADS

D_FF = 2816
FFN_CONV = 3

ALPHA = (2.0 * DEPTH) ** 0.25
BETA = (8.0 * DEPTH) ** -0.25

N_A = (DEPTH + 2) // 3
N_B = (DEPTH + 1) // 3
N_C = DEPTH // 3

kernel_name = "interleaved_swa_rglru_mla_deepnorm_trunk"


def layer_norm(x, g, b):
    xf = x.astype(jnp.float32)
    mu = jnp.mean(xf, axis=-1, keepdims=True)
    var = jnp.mean(jnp.square(xf - mu), axis=-1, keepdims=True)
    y = (xf - mu) * lax.rsqrt(var + LN_EPS) * g.astype(jnp.float32) + b.astype(jnp.float32)
    return y.astype(x.dtype)


def rms_norm(x, g):
    xf = x.astype(jnp.float32)
    y = xf * lax.rsqrt(jnp.mean(jnp.square(xf), axis=-1, keepdims=True) + RMS_EPS)
    return (y * g.astype(jnp.float32)).astype(x.dtype)


def rope_tables(seq, dim):
    inv = 1.0 / (ROPE_THETA ** (jnp.arange(0, dim, 2, dtype=jnp.float32) / dim))
    ang = jnp.arange(seq, dtype=jnp.float32)[:, None] * inv[None, :]
    return jnp.cos(ang), jnp.sin(ang)


def apply_rope(x, cos, sin):
    c = cos[None, :, None, :].astype(x.dtype)
    s = sin[None, :, None, :].astype(x.dtype)
    x1, x2 = jnp.split(x, 2, axis=-1)
    return jnp.concatenate([x1 * c - x2 * s, x2 * c + x1 * s], axis=-1)


def causal_depthwise_conv(x, w, b):
    k_width = w.shape[0]
    s = x.shape[1]
    xp = jnp.pad(x, ((0, 0), (k_width - 1, 0), (0, 0)))
    y = xp[:, 0:s] * w[0]
    for k in range(1, k_width):
        y = y + xp[:, k:k + s] * w[k]
    return y + b


def swa_sink_attention(x, w_qkv, sinks, w_o, cos, sin):
    bsz, s, _ = x.shape
    grp = A_HEADS // A_KV_HEADS
    nb = s // BLOCK
    qkv = x @ w_qkv
    q, k, v = jnp.split(qkv, [A_HEADS * A_HEAD_DIM, (A_HEADS + A_KV_HEADS) * A_HEAD_DIM], axis=-1)
    q = apply_rope(q.reshape(bsz, s, A_HEADS, A_HEAD_DIM), cos, sin)
    k = apply_rope(k.reshape(bsz, s, A_KV_HEADS, A_HEAD_DIM), cos, sin)
    v = v.reshape(bsz, s, A_KV_HEADS, A_HEAD_DIM)
    qb = q.reshape(bsz, nb, BLOCK, A_KV_HEADS, grp, A_HEAD_DIM)

    def with_prev(t):
        tb = t.reshape(bsz, nb, BLOCK, A_KV_HEADS, A_HEAD_DIM)
        prev = jnp.pad(tb[:, :-1], ((0, 0), (1, 0), (0, 0), (0, 0), (0, 0)))
        return jnp.concatenate([prev, tb], axis=2)

    kb, vb = with_prev(k), with_prev(v)
    scores = jnp.einsum('bnqhgd,bnkhd->bnhgqk', qb, kb).astype(jnp.float32) * (A_HEAD_DIM ** -0.5)
    qi = jnp.arange(BLOCK)[:, None]
    kj = jnp.arange(2 * BLOCK)[None, :]
    dist = qi + BLOCK - kj
    band = (dist >= 0) & (dist < A_WINDOW)
    real_key = (jnp.arange(nb)[:, None, None] > 0) | (kj >= BLOCK)[None]
    valid = band[None] & real_key
    scores = jnp.where(valid[None, :, None, None], scores, NEG)
    sink = sinks.astype(jnp.float32).reshape(A_KV_HEADS, grp)[None, None, :, :, None, None]
    sink = jnp.broadcast_to(sink, scores.shape[:-1] + (1,))
    probs = jax.nn.softmax(jnp.concatenate([scores, sink], axis=-1), axis=-1)[..., :-1]
    out = jnp.einsum('bnhgqk,bnkhd->bnqhgd', probs.astype(v.dtype), vb)
    return out.reshape(bsz, s, A_HEADS * A_HEAD_DIM) @ w_o


def rglru_block(x, w_in, conv_w, conv_b, w_rgate, b_rgate, w_igate, b_igate, lam, w_o):
    bsz, s, _ = x.shape
    gate, u = jnp.split(x @ w_in, 2, axis=-1)
    u = causal_depthwise_conv(u, conv_w, conv_b)
    ub = u.reshape(bsz, s, LRU_BLOCKS, LRU_BLOCK_W)
    r = jax.nn.sigmoid(jnp.einsum('bshi,hij->bshj', ub, w_rgate).reshape(bsz, s, LRU_WIDTH) + b_rgate)
    i = jax.nn.sigmoid(jnp.einsum('bshi,hij->bshj', ub, w_igate).reshape(bsz, s, LRU_WIDTH) + b_igate)
    log_a = -LRU_C * r.astype(jnp.float32) * jax.nn.softplus(-lam.astype(jnp.float32))
    a = jnp.exp(log_a)
    b_in = jnp.sqrt(-jnp.expm1(2.0 * log_a)) * (i * u).astype(jnp.float32)

    def combine(c1, c2):
        a1, b1 = c1
        a2, b2 = c2
        return a1 * a2, a2 * b1 + b2

    _, h = lax.associative_scan(combine, (a, b_in), axis=1)
    y = h.astype(x.dtype) * jax.nn.gelu(gate)
    return y @ w_o


def mla_attention(x, w_down, q_norm, kv_norm, w_uq, w_ukv, w_o, cos_r, sin_r):
    bsz, s, _ = x.shape
    nb = s // BLOCK
    c = x @ w_down
    cq, ckv, k_rope = jnp.split(c, [C_Q_RANK, C_Q_RANK + C_KV_RANK], axis=-1)
    cq = rms_norm(cq, q_norm)
    ckv = rms_norm(ckv, kv_norm)
    q = (cq @ w_uq).reshape(bsz, s, C_HEADS, C_NOPE + C_ROPE)
    q_nope, q_rope = jnp.split(q, [C_NOPE], axis=-1)
    q_rope = apply_rope(q_rope, cos_r, sin_r)
    k_rope = apply_rope(k_rope[:, :, None, :], cos_r, sin_r)[:, :, 0]
    kv = (ckv @ w_ukv).reshape(bsz, s, C_HEADS, C_NOPE + C_V)
    k_nope, v = jnp.split(kv, [C_NOPE], axis=-1)
    scale = (C_NOPE + C_ROPE) ** -0.5
    qn = q_nope.reshape(bsz, nb, BLOCK, C_HEADS, C_NOPE).transpose(1, 0, 2, 3, 4)
    qr = q_rope.reshape(bsz, nb, BLOCK, C_HEADS, C_ROPE).transpose(1, 0, 2, 3, 4)
    key_pos = jnp.arange(s)

    def attend(args):
        n, qn_b, qr_b = args
        sc = (jnp.einsum('bqhd,bkhd->bhqk', qn_b, k_nope)
              + jnp.einsum('bqhd,bkd->bhqk', qr_b, k_rope)).astype(jnp.float32) * scale
        q_pos = n * BLOCK + jnp.arange(BLOCK)
        sc = jnp.where(key_pos[None, :] <= q_pos[:, None], sc, NEG)
        p = jax.nn.softmax(sc, axis=-1)
        return jnp.einsum('bhqk,bkhd->bqhd', p.astype(v.dtype), v)

    out = lax.map(attend, (jnp.arange(nb), qn, qr))
    out = out.transpose(1, 0, 2, 3, 4).reshape(bsz, s, C_HEADS * C_V)
    return out @ w_o


def memory_cross_attention(x, mem_k, mem_v, w_q, w_o):
    bsz, s, _ = x.shape
    q = (x @ w_q).reshape(bsz, s, X_HEADS, X_HEAD_DIM)
    sc = jnp.einsum('bshd,bmhd->bhsm', q, mem_k).astype(jnp.float32) * (X_HEAD_DIM ** -0.5)
    p = jax.nn.softmax(sc, axis=-1)
    o = jnp.einsum('bhsm,bmhd->bshd', p.astype(mem_v.dtype), mem_v).reshape(bsz, s, D_MODEL)
    return o @ w_o


def conv_glu_ffn(x, w_up, conv_w, conv_b, w_down):
    h = causal_depthwise_conv(x @ w_up, conv_w, conv_b)
    g, u = jnp.split(h, 2, axis=-1)
    return (jax.nn.silu(g) * u) @ w_down


def setup_inputs(seed: int = 0) -> dict:
    key = jax.random.key(seed)
    ks = iter(jax.random.split(key, 40))

    def dense(shape, fan_in, scale=1.0):
        return jax.random.normal(next(ks), shape, jnp.float32) * (scale * fan_in ** -0.5)

    def small(shape, scale=0.01):
        return jax.random.normal(next(ks), shape, jnp.float32) * scale

    def gain(shape):
        return 1.0 + small(shape)

    qkv_w = (A_HEADS + 2 * A_KV_HEADS) * A_HEAD_DIM
    a0 = jax.random.uniform(next(ks), (N_B, LRU_WIDTH), jnp.float32, 0.9, 0.999) ** (1.0 / LRU_C)
    return {
        "x": jax.random.normal(next(ks), (BATCH, SEQ, D_MODEL), jnp.float32),
        "mem": jax.random.normal(next(ks), (BATCH, MEM_LEN, D_MODEL), jnp.float32),
        "a_w_qkv": dense((N_A, D_MODEL, qkv_w), D_MODEL),
        "a_sinks": small((N_A, A_HEADS), 1.0),
        "a_w_o": dense((N_A, A_HEADS * A_HEAD_DIM, D_MODEL), A_HEADS * A_HEAD_DIM, BETA),
        "b_w_in": dense((N_B, D_MODEL, 2 * LRU_WIDTH), D_MODEL),
        "b_conv_w": dense((N_B, LRU_CONV, LRU_WIDTH), LRU_CONV),
        "b_conv_b": small((N_B, LRU_WIDTH)),
        "b_w_rgate": dense((N_B, LRU_BLOCKS, LRU_BLOCK_W, LRU_BLOCK_W), LRU_BLOCK_W),
        "b_b_rgate": small((N_B, LRU_WIDTH)),
        "b_w_igate": dense((N_B, LRU_BLOCKS, LRU_BLOCK_W, LRU_BLOCK_W), LRU_BLOCK_W),
        "b_b_igate": small((N_B, LRU_WIDTH)),
        "b_lambda": jnp.log(a0) - jnp.log1p(-a0),
        "b_w_o": dense((N_B, LRU_WIDTH, D_MODEL), LRU_WIDTH, BETA),
        "c_w_down": dense((N_C, D_MODEL, C_Q_RANK + C_KV_RANK + C_ROPE), D_MODEL),
        "c_q_norm": gain((N_C, C_Q_RANK)),
        "c_kv_norm": gain((N_C, C_KV_RANK)),
        "c_w_uq": dense((N_C, C_Q_RANK, C_HEADS * (C_NOPE + C_ROPE)), C_Q_RANK),
        "c_w_ukv": dense((N_C, C_KV_RANK, C_HEADS * (C_NOPE + C_V)), C_KV_RANK),
        "c_w_o": dense((N_C, C_HEADS * C_V, D_MODEL), C_HEADS * C_V, BETA),
        "mem_w_kv": dense((D_MODEL, 2 * D_MODEL), D_MODEL),
        "x_w_q": dense((DEPTH, D_MODEL, D_MODEL), D_MODEL),
        "x_w_o": dense((DEPTH, D_MODEL, D_MODEL), D_MODEL, BETA),
        "f_w_up": dense((DEPTH, D_MODEL, 2 * D_FF), D_MODEL),
        "f_conv_w": dense((DEPTH, FFN_CONV, 2 * D_FF), FFN_CONV),
        "f_conv_b": small((DEPTH, 2 * D_FF)),
        "f_w_down": dense((DEPTH, D_FF, D_MODEL), D_FF, BETA),
        "ln_g": gain((DEPTH, 3, D_MODEL)),
        "ln_b": small((DEPTH, 3, D_MODEL)),
    }


def reference(x, mem, a_w_qkv, a_sinks, a_w_o, b_w_in, b_conv_w, b_conv_b, b_w_rgate, b_b_rgate,
              b_w_igate, b_b_igate, b_lambda, b_w_o, c_w_down, c_q_norm, c_kv_norm, c_w_uq, c_w_ukv,
              c_w_o, mem_w_kv, x_w_q, x_w_o, f_w_up, f_conv_w, f_conv_b, f_w_down, ln_g, ln_b):
    bsz, s, _ = x.shape
    cos_a, sin_a = rope_tables(s, A_HEAD_DIM)
    cos_c, sin_c = rope_tables(s, C_ROPE)
    mem_k, mem_v = jnp.split(mem @ mem_w_kv, 2, axis=-1)
    mem_k = mem_k.reshape(bsz, MEM_LEN, X_HEADS, X_HEAD_DIM)
    mem_v = mem_v.reshape(bsz, MEM_LEN, X_HEADS, X_HEAD_DIM)
    for i in range(DEPTH):
        kind, j = i % N_MIXERS, i // N_MIXERS
        if kind == 0:
            y = swa_sink_attention(x, a_w_qkv[j], a_sinks[j], a_w_o[j], cos_a, sin_a)
        elif kind == 1:
            y = rglru_block(x, b_w_in[j], b_conv_w[j], b_conv_b[j], b_w_rgate[j], b_b_rgate[j],
                            b_w_igate[j], b_b_igate[j], b_lambda[j], b_w_o[j])
        else:
            y = mla_attention(x, c_w_down[j], c_q_norm[j], c_kv_norm[j], c_w_uq[j], c_w_ukv[j],
                              c_w_o[j], cos_c, sin_c)
        x = layer_norm(ALPHA * x + y, ln_g[i, 0], ln_b[i, 0])
        x = layer_norm(ALPHA * x + memory_cross_attention(x, mem_k, mem_v, x_w_q[i], x_w_o[i]),
                       ln_g[i, 1], ln_b[i, 1])
        x = layer_norm(ALPHA * x + conv_glu_ffn(x, f_w_up[i], f_conv_w[i], f_conv_b[i], f_w_down[i]),
                       ln_g[i, 2], ln_b[i, 2])
    return x
```

```python
import math
import numpy as np
from contextlib import ExitStack
import concourse.bass as bass
import concourse.mybir as mybir
from concourse.bass_utils import run_bass_kernel_spmd

F32 = mybir.dt.float32
BF16 = mybir.dt.bfloat16
I32 = mybir.dt.int32
AF = mybir.ActivationFunctionType
ALU = mybir.AluOpType

D = 1024
DEPTH = 4
MEM = 256
DFF = 2816
ALPHA = 8.0 ** 0.25
LN_EPS = 1e-5
RMS_EPS = 1e-6
N_CORES = 8

WSHAPES = [
    ("a_w_qkv", (2, 1024, 1536)), ("a_sinks", (2, 16)), ("a_w_o", (2, 1024, 1024)),
    ("b_w_in", (1, 1024, 2048)), ("b_conv_w", (1, 4, 1024)), ("b_conv_b", (1, 1024)),
    ("b_w_rgate", (1, 4, 256, 256)), ("b_b_rgate", (1, 1024)), ("b_w_igate", (1, 4, 256, 256)),
    ("b_b_igate", (1, 1024)), ("b_lambda", (1, 1024)), ("b_w_o", (1, 1024, 1024)),
    ("c_w_down", (1, 1024, 704)), ("c_q_norm", (1, 384)), ("c_kv_norm", (1, 256)),
    ("c_w_uq", (1, 384, 1536)), ("c_w_ukv", (1, 256, 2048)), ("c_w_o", (1, 1024, 1024)),
    ("mem_w_kv", (1024, 2048)), ("x_w_q", (4, 1024, 1024)), ("x_w_o", (4, 1024, 1024)),
    ("f_w_up", (4, 1024, 5632)), ("f_conv_w", (4, 3, 5632)), ("f_conv_b", (4, 5632)),
    ("f_w_down", (4, 2816, 1024)), ("ln_g", (4, 3, 1024)), ("ln_b", (4, 3, 1024)),
]


class Buf:
    __slots__ = ("w", "r")

    def __init__(self):
        self.w = None
        self.r = {}


class Prog:
    CE = ("pe", "act", "dve", "pool")
    NDMA = 8

    def __init__(self, nc, es):
        self.nc = nc
        self.es = es
        self.eng = dict(pe=nc.tensor, act=nc.scalar, dve=nc.vector, pool=nc.gpsimd, sp=nc.sync)
        self.sems = {}
        self.cnt = {}
        for e in self.CE:
            self.sems[e] = es.enter_context(nc.semaphore("s_" + e))
            self.cnt[e] = 0
        self.dq = {}
        self.waited = {e: {} for e in self.eng}
        self.nins = 0

    def _dq(self, q):
        if q not in self.dq:
            keys = []
            for i in range(self.NDMA):
                k = "d_%s_%d" % (q, i)
                self.sems[k] = self.es.enter_context(self.nc.semaphore(k))
                keys.append(k)
            self.dq[q] = [keys, 0]
        return self.dq[q]

    def _wait(self, e, key, val):
        if self.waited[e].get(key, 0) >= val:
            return
        self.eng[e].wait_ge(self.sems[key], val)
        self.waited[e][key] = val
        self.nins += 1

    def _deps(self, e, mykey, reads, writes):
        deps = {}
        for b in reads:
            if b.w is not None:
                k, v = b.w
                if not (k == mykey and e == "pe"):
                    if deps.get(k, 0) < v:
                        deps[k] = v
        same_ok = (e == "pe") or (e not in self.CE)
        for b in writes:
            if b.w is not None:
                k, v = b.w
                if not (k == mykey and same_ok) and deps.get(k, 0) < v:
                    deps[k] = v
            for k, v in b.r.items():
                if not (k == mykey and same_ok) and deps.get(k, 0) < v:
                    deps[k] = v
        for k, v in deps.items():
            self._wait(e, k, v)

    def _mark(self, key, val, reads, writes):
        for b in writes:
            b.w = (key, val)
            b.r = {}
        for b in reads:
            if b.r.get(key, 0) < val:
                b.r[key] = val

    def op(self, e, fn, reads=(), writes=(), sig=True):
        self._deps(e, e, reads, writes)
        ins = fn(self.eng[e])
        if sig:
            self.cnt[e] += 1
            ins.then_inc(self.sems[e], 1)
            val = self.cnt[e]
        else:
            val = self.cnt[e] + 1
        self._mark(e, val, reads, writes)
        self.nins += 1
        return ins

    def dma(self, q, out, in_, reads=(), writes=(), **kw):
        keys, n = self._dq(q)
        key = keys[n % self.NDMA]
        rnd = n // self.NDMA
        if rnd > 0:
            self._wait(q, key, 16 * rnd)
        self._deps(q, key, reads, writes)
        ins = self.eng[q].dma_start(out=out, in_=in_, **kw)
        ins.then_inc(self.sems[key], 16)
        self.dq[q][1] = n + 1
        self._mark(key, 16 * (rnd + 1), reads, writes)
        self.nins += 1
        return ins

    def totals(self):
        tot = {e: self.cnt[e] for e in self.CE}
        for q, (keys, n) in self.dq.items():
            for i, k in enumerate(keys):
                c = (n - i + self.NDMA - 1) // self.NDMA
                if c > 0:
                    tot[k] = 16 * c
        return tot

    def barrier(self, engines=None):
        tot = self.totals()
        for e in (engines or list(self.eng)):
            for k, v in tot.items():
                if v > 0 and k != e:
                    self._wait(e, k, v)

    def join(self, engines, bufs):
        for e in engines:
            for b in bufs:
                if b.w is not None and b.w[0] != e:
                    self._wait(e, b.w[0], b.w[1])


def build(T, upto=12, start=0):
    NB = T // 128
    assert T % 512 == 0
    nc = bass.Bass("TRN2", target_bir_lowering=False)

    def din(name, shape):
        return nc.dram_tensor(name, list(shape), F32, kind="ExternalInput").ap()

    x_in = din("x", [T, D])
    mem_in = din("mem", [MEM, D])
    W = {n: din(n, s) for n, s in WSHAPES}
    c_ident = din("c_ident", [128, 128])
    c_mask = din("c_mask", [128, 256])
    c_pos = din("c_pos", [128, NB])
    c_iota = din("c_iota", [128, 32])
    out_d = nc.dram_tensor("out", [T, D], F32, kind="ExternalOutput").ap()
    xs_d = nc.dram_tensor("xs_scr", [T, D], F32).ap()
    rope_d = nc.dram_tensor("rope_scr", [128, NB, 64], F32).ap()
    cqnT_d = nc.dram_tensor("cqnT_scr", [3, 128, T], BF16).ap()
    qrT_d = nc.dram_tensor("qrT_scr", [8, 64, T], BF16).ap()
    v_d = nc.dram_tensor("v_scr", [T, 1024], BF16).ap()
    attT_d = nc.dram_tensor("attT_scr", [8, 128, T], BF16).ap()
    memKT_d = nc.dram_tensor("memKT_scr", [128, 2048], BF16).ap()
    memV_d = nc.dram_tensor("memV_scr", [128, 2048], BF16).ap()

    ges = ExitStack()
    P = Prog(nc, ges)
    uid = [0]

    def sbt(es, shape, dt=F32):
        uid[0] += 1
        return es.enter_context(nc.sbuf_tensor("t%d" % uid[0], list(shape), dt))

    PSB = [ges.enter_context(nc.psum_tensor("psb%d" % i, [128, 512], F32)) for i in range(8)]
    BPS = [Buf() for _ in range(8)]

    def psf(i):
        return PSB[i][:]

    def psh(i):
        return PSB[i][:].bitcast(BF16)

    ident_f = sbt(ges, [128, 128])
    ident = sbt(ges, [128, 128], BF16)
    mask_f = sbt(ges, [128, 256])
    mask = sbt(ges, [128, 256], BF16)
    ones = sbt(ges, [128, 128], BF16)
    wstg = [sbt(ges, [128, 1024]) for _ in range(3)]
    bwstg = [Buf() for _ in range(3)]
    wcnt = [0]
    xs_b = [Buf() for _ in range(NB)]
    out_b = [Buf() for _ in range(NB)]

    def cast(e, out, in_, reads, writes):
        if e == "act":
            P.op("act", lambda g: g.copy(out, in_), reads=reads, writes=writes)
        else:
            P.op(e, lambda g: g.tensor_copy(out, in_), reads=reads, writes=writes)

    CAST_ENG = ("act", "dve", "pool")

    def load_cast(dst, src, wb):
        shape = list(dst.shape)
        npart = shape[0]
        n = 1
        for s_ in shape[1:]:
            n *= s_
        assert n <= 1024, shape
        i = wcnt[0] % 3
        wcnt[0] += 1
        view = wstg[i][0:npart, 0:n]
        if len(shape) == 3:
            view = view.rearrange("p (a b) -> p a b", a=shape[1])
        P.dma("sp", view, src, writes=[bwstg[i]])
        b = Buf()
        cast(CAST_ENG[i], dst, view, [bwstg[i]], [b])
        wb.append(b)

    def load_w_rows(dst2d, src2d, nchunk, ncols, wb, row0=0):
        for c in range(nchunk):
            for j0 in range(0, ncols, 1024):
                w_ = min(1024, ncols - j0)
                load_cast(dst2d[:, c * ncols + j0: c * ncols + j0 + w_],
                          src2d[row0 + c * 128: row0 + (c + 1) * 128, j0:j0 + w_], wb)

    ALLC = ("pe", "act", "dve", "pool")

    with ExitStack() as es:
        b0 = Buf()
        P.dma("sp", ident_f[:], c_ident[:, :], writes=[b0])
        bi = Buf()
        cast("dve", ident[:], ident_f[:], [b0], [bi])
        b1 = Buf()
        P.dma("sp", mask_f[:], c_mask[:, :], writes=[b1])
        bm = Buf()
        cast("dve", mask[:], mask_f[:], [b1], [bm])
        bo = Buf()
        P.op("pool", lambda g: g.memset(ones[:], 1.0), writes=[bo])
        pos = sbt(es, [128, NB])
        iot = sbt(es, [128, 32])
        bp = Buf()
        bio = Buf()
        P.dma("sp", pos[:], c_pos[:, :], writes=[bp])
        P.dma("sp", iot[:], c_iota[:, :], writes=[bio])
        inv = sbt(es, [128, 32])
        binv = Buf()
        P.op("act", lambda g: g.activation(inv[:], iot[:], AF.Exp, scale=-math.log(10000.0) / 32.0),
             reads=[bio], writes=[binv])
        ytab = sbt(es, [128, NB * 64])
        by = Buf()
        P.op("dve", lambda g: g.tensor_scalar(inv[:], inv[:], float(1.0 / (2 * math.pi)), None, ALU.mult),
             reads=[binv], writes=[binv])
        for j in range(NB):
            P.op("dve", lambda g: g.tensor_scalar(ytab[:, j * 64 + 32: j * 64 + 64], inv[:], pos[:, j:j + 1],
                                                   None, ALU.mult),
                 reads=[binv, bp], writes=[by])
            P.op("dve", lambda g: g.tensor_scalar(ytab[:, j * 64: j * 64 + 32], ytab[:, j * 64 + 32: j * 64 + 64],
                                                   0.25, None, ALU.add),
                 reads=[by], writes=[by])
        yi = sbt(es, [128, NB * 64], I32)
        byi = Buf()
        P.op("dve", lambda g: g.tensor_copy(yi[:], ytab[:]), reads=[by], writes=[byi])
        yf = sbt(es, [128, NB * 64])
        byf = Buf()
        P.op("dve", lambda g: g.tensor_copy(yf[:], yi[:]), reads=[byi], writes=[byf])
        P.op("dve", lambda g: g.tensor_tensor(ytab[:], ytab[:], yf[:], ALU.subtract), reads=[by, byf], writes=[by])
        P.op("act", lambda g: g.activation(yf[:], ytab[:], AF.Sin, scale=6.283185), reads=[by], writes=[byf])
        brope = Buf()
        P.dma("sp", rope_d.rearrange("p j c -> p (j c)"), yf[:], reads=[byf], writes=[brope])
        memKT = sbt(es, [128, 8 * 256], BF16)
        memV = sbt(es, [128, 2 * 1024], BF16)
        wkv = sbt(es, [128, 8 * 2048], BF16)
        wb = []
        load_w_rows(wkv[:], W["mem_w_kv"], 8, 2048, wb)
        memf = sbt(es, [128, 2 * 1024])
        bmf = Buf()
        P.dma("sp", memf[:].rearrange("p (s f) -> p s f", s=2),
              mem_in.rearrange("(s p) f -> p s f", p=128), writes=[bmf])
        memb = sbt(es, [128, 2 * 1024], BF16)
        bmb = Buf()
        cast("pool", memb[:], memf[:], [bmf], [bmb])
        memT = sbt(es, [128, 8 * 256], BF16)
        bmT = Buf()
        for s in range(2):
            for c in range(8):
                P.op("pe", lambda g: g.transpose(psh(0)[:, c * 128:(c + 1) * 128],
                                                 memb[:, s * 1024 + c * 128: s * 1024 + (c + 1) * 128], ident[:]),
                     reads=[bmb, bi], writes=[BPS[0]], sig=(c == 7))
            P.op("act", lambda g: g.copy(
                memT[:].rearrange("p (c t) -> p c t", c=8)[:, :, s * 128:(s + 1) * 128],
                psh(0).rearrange("p (c t) -> p c t", c=8)), reads=[BPS[0]], writes=[bmT])
        P.join(["pe"], wb)
        bk = Buf()
        for m in range(8):
            pb = 1 + (m % 2)
            for k in range(8):
                P.op("pe", lambda g: g.matmul(psf(pb)[:, 0:256], wkv[:, k * 2048 + m * 128: k * 2048 + (m + 1) * 128],
                                              memT[:, k * 256:(k + 1) * 256], start=(k == 0), stop=(k == 7)),
                     reads=[bmT], writes=[BPS[pb]], sig=(k == 7))
            P.op("act", lambda g: g.copy(memKT[:, m * 256:(m + 1) * 256], psf(pb)[:, 0:256]),
                 reads=[BPS[pb]], writes=[bk])
        for mc in range(2):
            for n in range(2):
                pb = 3 + n
                for k in range(8):
                    P.op("pe", lambda g: g.matmul(psf(pb), memT[:, k * 256 + mc * 128: k * 256 + (mc + 1) * 128],
                                                  wkv[:, k * 2048 + 1024 + n * 512: k * 2048 + 1024 + (n + 1) * 512],
                                                  start=(k == 0), stop=(k == 7)),
                         reads=[bmT], writes=[BPS[pb]], sig=(k == 7))
                P.op("act", lambda g: g.copy(memV[:, mc * 1024 + n * 512: mc * 1024 + (n + 1) * 512], psf(pb)),
                     reads=[BPS[pb]], writes=[bk])
        bmem = Buf()
        P.dma("sp", memKT_d[:, :], memKT[:], reads=[bk], writes=[bmem])
        P.dma("sp", memV_d[:, :], memV[:], reads=[bk], writes=[bmem])
        P.barrier()

    def load_ln(es, li, j):
        gam = sbt(es, [128, 1024])
        bet = sbt(es, [128, 1024])
        epsc = sbt(es, [128, 1])
        b = Buf()
        P.dma("sp", gam[:], W["ln_g"][li, j].partition_broadcast(128), writes=[b])
        P.dma("sp", bet[:], W["ln_b"][li, j].partition_broadcast(128), writes=[b])
        P.op("pool", lambda g: g.memset(epsc[:], LN_EPS), writes=[b])
        return gam, bet, epsc, b

    class LNState:
        pass

    def make_ln(es, li, j, nz=2):
        st = LNState()
        st.gam, st.bet, st.eps, st.bpar = load_ln(es, li, j)
        st.z = [sbt(es, [128, 1024]) for _ in range(nz)]
        st.bz = [Buf() for _ in range(nz)]
        st.stats = [sbt(es, [128, 16]) for _ in range(2)]
        st.bst = [Buf() for _ in range(2)]
        st.n = 0
        st.pending = []
        return st

    def ln_flush(st, keep=0):
        while len(st.pending) > keep:
            st.pending.pop(0)()

    def ln_epilogue(st, py, xres, bxres, dst, bdst):
        i = st.n % 2
        st.n += 1
        z, bz, sx, bs = st.z[i % len(st.z)], st.bz[i % len(st.z)], st.stats[i], st.bst[i]
        for n in range(2):
            P.op("dve", lambda g: g.scalar_tensor_tensor(z[:, n * 512:(n + 1) * 512], xres[:, n * 512:(n + 1) * 512],
                                                         float(ALPHA), psf(py[n]), ALU.mult, ALU.add),
                 reads=[bxres, BPS[py[n]]], writes=[bz])
        for n in range(2):
            P.op("dve", lambda g: g.bn_stats(sx[:, n * 6:(n + 1) * 6], z[:, n * 512:(n + 1) * 512]),
                 reads=[bz], writes=[bs])
        P.op("dve", lambda g: g.bn_aggr(sx[:, 12:14], sx[:, 0:12]), reads=[bs], writes=[bs])
        P.op("act", lambda g: g.activation(sx[:, 14:15], sx[:, 13:14], AF.Ln, bias=st.eps[:, 0:1], scale=1.0),
             reads=[bs, st.bpar], writes=[bs])
        P.op("act", lambda g: g.activation(sx[:, 14:15], sx[:, 14:15], AF.Exp, scale=-0.5), reads=[bs], writes=[bs])
        def finish():
            P.op("dve", lambda g: g.scalar_tensor_tensor(sx[:, 15:16], sx[:, 12:13], -1.0, sx[:, 14:15], ALU.mult, ALU.mult),
                 reads=[bs], writes=[bs])
            P.op("act", lambda g: g.activation(z[:], z[:], AF.Identity, bias=sx[:, 15:16], scale=sx[:, 14:15]),
                 reads=[bz, bs], writes=[bz])
            P.op("pool", lambda g: g.tensor_tensor(z[:], z[:], st.gam[:], ALU.mult), reads=[bz, st.bpar], writes=[bz])
            P.op("pool", lambda g: g.tensor_tensor(z[:], z[:], st.bet[:], ALU.add), reads=[bz, st.bpar], writes=[bz])
            P.dma("sp", dst, z[:], reads=[bz], writes=[bdst])

        st.pending.append(finish)
        ln_flush(st, len(st.z) - 1)

    class XT:
        pass

    def make_xt(es, nsub, nbuf=2, nxt=2, nxb=2):
        o = XT()
        o.nsub = nsub
        o.xin = [sbt(es, [128, nsub * 1024]) for _ in range(nbuf)]
        o.bxin = [Buf() for _ in range(nbuf)]
        o.xb = [sbt(es, [128, 1024], BF16) for _ in range(nxb)]
        o.bxb = [Buf() for _ in range(nxb)]
        o.ncast = 0
        o.xT = [sbt(es, [128, 8 * nsub * 128], BF16) for _ in range(nxt)]
        o.bxT = [Buf() for _ in range(nxt)]
        o.n = 0
        o.nissue = 0
        o.issued = {}
        return o

    def issue_xt(o, src, src_bufs, sub0):
        if sub0 in o.issued or sub0 * 128 >= T:
            return
        i = o.nissue % len(o.xin)
        o.nissue += 1
        o.issued[sub0] = i
        ns = o.nsub
        P.dma("sp", o.xin[i][:].rearrange("p (s f) -> p s f", s=ns),
              src[sub0 * 128:(sub0 + ns) * 128, :].rearrange("(s p) f -> p s f", p=128),
              reads=[src_bufs[sub0 + s] for s in range(ns)] if src_bufs else [], writes=[o.bxin[i]])

    def prep_xt(o, src, src_bufs, sub0, trbank, prefetch=True):
        issue_xt(o, src, src_bufs, sub0)
        i = o.issued[sub0]
        o.n += 1
        ns = o.nsub
        xin, bxin = o.xin[i], o.bxin[i]
        xT, bxT = o.xT[(o.n - 1) % len(o.xT)], o.bxT[(o.n - 1) % len(o.xT)]
        if len(o.xin) > 1 and prefetch:
            issue_xt(o, src, src_bufs, sub0 + ns)
        W_ = ns * 128
        for s in range(ns):
            xb_, bxb_ = o.xb[o.ncast % len(o.xb)], o.bxb[o.ncast % len(o.xb)]
            o.ncast += 1
            cast("dve", xb_[:], xin[:, s * 1024:(s + 1) * 1024], [bxin], [bxb_])
            for c in range(8):
                P.op("pe", lambda g: g.transpose(psh(trbank)[:, c * 128:(c + 1) * 128],
                                                 xb_[:, c * 128:(c + 1) * 128], ident[:]),
                     reads=[bxb_], writes=[BPS[trbank]], sig=(c == 7))
            P.op("act", lambda g: g.copy(
                xT[:].rearrange("p (c t) -> p c t", c=8)[:, :, s * 128:(s + 1) * 128],
                psh(trbank).rearrange("p (c t) -> p c t", c=8)), reads=[BPS[trbank]], writes=[bxT])
        return xin, bxin, xT, bxT, W_

    def dst_for(sub_idx, last):
        if last:
            return out_d[sub_idx * 128:(sub_idx + 1) * 128, :], out_b[sub_idx]
        return xs_d[sub_idx * 128:(sub_idx + 1) * 128, :], xs_b[sub_idx]

    def phase_xattn(li, src, src_bufs, last):
        with ExitStack() as es:
            wq = sbt(es, [128, 8 * 1024], BF16)
            wo = sbt(es, [128, 8 * 1024], BF16)
            wb = []
            load_w_rows(wq[:], W["x_w_q"][li], 8, 1024, wb)
            load_w_rows(wo[:], W["x_w_o"][li], 8, 1024, wb)
            memKT = sbt(es, [128, 8 * 256], BF16)
            memV = sbt(es, [128, 2 * 1024], BF16)
            bmkv = Buf()
            P.dma("sp", memKT[:], memKT_d[:, :], reads=[bmem], writes=[bmkv])
            P.dma("sp", memV[:], memV_d[:, :], reads=[bmem], writes=[bmkv])
            wb.append(bmkv)
            ln = make_ln(es, li, 1)
            xt = make_xt(es, 4)
            qT = sbt(es, [128, 8 * 512], BF16)
            bq = Buf()
            pT = [sbt(es, [128, 512], BF16) for _ in range(4)]
            bpT = [Buf() for _ in range(4)]
            rden = [sbt(es, [128, 512]) for _ in range(2)]
            brd = [Buf(), Buf()]
            oT = sbt(es, [128, 8 * 512], BF16)
            boT = Buf()
            P.join(["pe"], wb)
            npt = 0
            nxt_prep = prep_xt(xt, src, src_bufs, 0, 0, prefetch=False)
            issue_xt(xt, src, src_bufs, 4)
            for t in range(NB // 4):
                xin, bxin, xT, bxT, Wd = nxt_prep
                for m in range(8):
                    pb = m % 2
                    for k in range(8):
                        P.op("pe", lambda g: g.matmul(psf(pb), wq[:, k * 1024 + m * 128: k * 1024 + (m + 1) * 128],
                                                      xT[:, k * 512:(k + 1) * 512], start=(k == 0), stop=(k == 7)),
                             reads=[bxT], writes=[BPS[pb]], sig=(k == 7))
                    cast("dve" if m % 2 else "act", qT[:, m * 512:(m + 1) * 512], psf(pb), [BPS[pb]], [bq])
                if t + 1 < NB // 4:
                    nxt_prep = prep_xt(xt, src, src_bufs, (t + 1) * 4, 0, prefetch=False)
                def emit_S(h):
                    nonlocal npt
                    pts = []
                    for mc in range(2):
                        pb = (2 + mc) if h % 2 == 0 else mc
                        for dc in range(2):
                            m = 2 * h + dc
                            P.op("pe", lambda g: g.matmul(psf(pb), memKT[:, m * 256 + mc * 128: m * 256 + (mc + 1) * 128],
                                                          qT[:, m * 512:(m + 1) * 512], start=(dc == 0), stop=(dc == 1)),
                                 reads=[bq], writes=[BPS[pb]], sig=(dc == 1))
                        pi = npt % 4
                        npt += 1
                        P.op("act", lambda g: g.activation(pT[pi][:], psf(pb), AF.Exp, scale=1.0 / 16.0),
                             reads=[BPS[pb]], writes=[bpT[pi]])
                        pts.append(pi)
                    return pts

                def emit_PV(h, pts):
                    for mc in range(2):
                        P.op("pe", lambda g: g.matmul(psf(4), ones[:], pT[pts[mc]][:], start=(mc == 0), stop=(mc == 1)),
                             reads=[bpT[pts[mc]]], writes=[BPS[4]], sig=(mc == 1))
                    P.op("act", lambda g: g.activation(rden[h % 2][:], psf(4), AF.Ln), reads=[BPS[4]], writes=[brd[h % 2]])
                    P.op("act", lambda g: g.activation(rden[h % 2][:], rden[h % 2][:], AF.Exp, scale=-1.0),
                         reads=[brd[h % 2]], writes=[brd[h % 2]])
                    for vc in range(2):
                        pb = 5 + vc
                        for mc in range(2):
                            P.op("pe", lambda g: g.matmul(
                                psf(pb), memV[:, mc * 1024 + h * 256 + vc * 128: mc * 1024 + h * 256 + (vc + 1) * 128],
                                pT[pts[mc]][:], start=(mc == 0), stop=(mc == 1)),
                                reads=[bpT[pts[mc]]], writes=[BPS[pb]], sig=(mc == 1))
                        m = 2 * h + vc
                        P.op("dve", lambda g: g.tensor_tensor(oT[:, m * 512:(m + 1) * 512], psf(pb), rden[h % 2][:], ALU.mult),
                             reads=[BPS[pb], brd[h % 2]], writes=[boT])

                pend = emit_S(0)
                for h in range(4):
                    nxt = emit_S(h + 1) if h + 1 < 4 else None
                    emit_PV(h, pend)
                    pend = nxt
                for s in range(4):
                    ybanks = (6, 7) if s % 2 == 0 else (4, 5)
                    for n in range(2):
                        pb = ybanks[n]
                        for k in range(8):
                            P.op("pe", lambda g: g.matmul(psf(pb), oT[:, k * 512 + s * 128: k * 512 + (s + 1) * 128],
                                                          wo[:, k * 1024 + n * 512: k * 1024 + (n + 1) * 512],
                                                          start=(k == 0), stop=(k == 7)),
                                 reads=[boT], writes=[BPS[pb]], sig=(k == 7))
                    dst, bdst = dst_for(t * 4 + s, last)
                    ln_epilogue(ln, ybanks, xin[:, s * 1024:(s + 1) * 1024], bxin, dst, bdst)
                issue_xt(xt, src, src_bufs, (t + 2) * 4)
            ln_flush(ln)
            P.barrier()

    def phase_ffn(li, src, src_bufs, last):
        with ExitStack() as es:
            wup = sbt(es, [128, 8 * 5632], BF16)
            wdn = sbt(es, [128, 22 * 1024], BF16)
            wb = []
            load_w_rows(wup[:], W["f_w_up"][li], 8, 5632, wb)
            load_w_rows(wdn[:], W["f_w_down"][li], 22, 1024, wb)
            cw = sbt(es, [128, 44 * 4])
            bcw = Buf()
            cw3 = cw[:].rearrange("p (m k) -> p m k", k=4)
            for k in range(3):
                P.dma("sp", cw3[:, :, k], W["f_conv_w"][li, k].rearrange("(m p) -> p m", p=128), writes=[bcw],
                      allow_slow_non_contiguous=True)
            P.dma("sp", cw3[:, :, 3], W["f_conv_b"][li].rearrange("(m p) -> p m", p=128), writes=[bcw],
                  allow_slow_non_contiguous=True)
            ln = make_ln(es, li, 2, nz=2)
            NS = 2
            TW = NS * 128
            xt = make_xt(es, NS, nbuf=2, nxt=1, nxb=1)
            hT = sbt(es, [128, 22 * TW], BF16)
            bhT = Buf()
            halo = sbt(es, [128, 44 * 2])
            bhalo = [Buf() for _ in range(44)]
            P.op("pool", lambda g: g.memset(halo[:], 0.0), writes=bhalo)
            stg = [sbt(es, [128, TW + 2]) for _ in range(4)]
            bstg = [Buf() for _ in range(4)]
            bstgh = [Buf() for _ in range(4)]
            NCV = 6
            cv = [sbt(es, [128, TW]) for _ in range(NCV)]
            bcv = [Buf() for _ in range(NCV)]
            P.join(ALLC, wb + [bcw])
            nxt_prep = prep_xt(xt, src, src_bufs, 0, 0, prefetch=False)
            issue_xt(xt, src, src_bufs, NS)
            for t in range(NB // NS):
                xin, bxin, xT, bxT, Wd = nxt_prep

                def S0(m):
                    for half in range(2):
                        mm = m + 22 * half
                        pb = 1 + ((2 * m + half) % 4)
                        for k in range(8):
                            P.op("pe", lambda g: g.matmul(psf(pb)[:, 0:TW],
                                                          wup[:, k * 5632 + mm * 128: k * 5632 + (mm + 1) * 128],
                                                          xT[:, k * TW:(k + 1) * TW], start=(k == 0), stop=(k == 7)),
                                 reads=[bxT], writes=[BPS[pb]], sig=(k == 7))

                def S1(m):
                    hs = []
                    for half in range(2):
                        mm = m + 22 * half
                        pb = 1 + ((2 * m + half) % 4)
                        ix = (2 * m + half) % 4
                        hs.append((mm, pb, stg[ix], bstg[ix], bstgh[ix], cv[(2 * m + half) % NCV], bcv[(2 * m + half) % NCV]))
                    for (mm, pb, sg, bsg, bsgh, c_, bc_) in hs:
                        P.op("pool", lambda g: g.tensor_copy(sg[:, 0:2], halo[:, mm * 2:mm * 2 + 2]),
                             reads=[bhalo[mm]], writes=[bsgh])
                    for (mm, pb, sg, bsg, bsgh, c_, bc_) in hs:
                        P.op("act", lambda g: g.copy(sg[:, 2:TW + 2], psf(pb)[:, 0:TW]), reads=[BPS[pb]], writes=[bsg])
                    for (mm, pb, sg, bsg, bsgh, c_, bc_) in hs:
                        P.op("pool", lambda g: g.tensor_copy(halo[:, mm * 2:mm * 2 + 2], sg[:, TW:TW + 2]),
                             reads=[bsg], writes=[bhalo[mm]])
                    for (mm, pb, sg, bsg, bsgh, c_, bc_) in hs:
                        P.op("act", lambda g: g.activation(c_[:], sg[:, 2:TW + 2], AF.Identity,
                                                           bias=cw[:, mm * 4 + 3: mm * 4 + 4],
                                                           scale=cw[:, mm * 4 + 2: mm * 4 + 3]),
                             reads=[bsg], writes=[bc_])

                def S2(m):
                    hs = []
                    for half in range(2):
                        mm = m + 22 * half
                        ix = (2 * m + half) % 4
                        hs.append((mm, stg[ix], bstg[ix], bstgh[ix], cv[(2 * m + half) % NCV], bcv[(2 * m + half) % NCV]))
                    for (mm, sg, bsg, bsgh, c_, bc_) in hs:
                        P.op("dve", lambda g: g.scalar_tensor_tensor(c_[:], sg[:, 1:TW + 1], cw[:, mm * 4 + 1: mm * 4 + 2],
                                                                     c_[:], ALU.mult, ALU.add),
                             reads=[bsg, bsgh, bc_], writes=[bc_])
                    for (mm, sg, bsg, bsgh, c_, bc_) in hs:
                        P.op("dve", lambda g: g.scalar_tensor_tensor(c_[:], sg[:, 0:TW], cw[:, mm * 4: mm * 4 + 1],
                                                                     c_[:], ALU.mult, ALU.add),
                             reads=[bsg, bsgh, bc_], writes=[bc_])

                def S3(m):
                    cg, bcg = cv[(2 * m) % NCV], bcv[(2 * m) % NCV]
                    cu, bcu = cv[(2 * m + 1) % NCV], bcv[(2 * m + 1) % NCV]
                    P.op("act", lambda g: g.activation(cg[:], cg[:], AF.Silu), reads=[bcg], writes=[bcg])
                    P.op("pool", lambda g: g.tensor_tensor(hT[:, m * TW:(m + 1) * TW], cg[:], cu[:], ALU.mult),
                         reads=[bcg, bcu], writes=[bhT])

                for i in range(22 + 3):
                    if i < 22:
                        S0(i)
                    if i == 22 and t + 1 < NB // NS:
                        nxt_prep = prep_xt(xt, src, src_bufs, (t + 1) * NS, 0, prefetch=False)
                    if 0 <= i - 1 < 22:
                        S1(i - 1)
                    if 0 <= i - 2 < 22:
                        S2(i - 2)
                    if 0 <= i - 3 < 22:
                        S3(i - 3)
                for s in range(NS):
                    ybanks = (5, 6) if s % 2 == 0 else (7, 0)
                    for n in range(2):
                        pb = ybanks[n]
                        for k in range(22):
                            P.op("pe", lambda g: g.matmul(psf(pb), hT[:, k * TW + s * 128: k * TW + (s + 1) * 128],
                                                          wdn[:, k * 1024 + n * 512: k * 1024 + (n + 1) * 512],
                                                          start=(k == 0), stop=(k == 21)),
                                 reads=[bhT], writes=[BPS[pb]], sig=(k == 21))
                    dst, bdst = dst_for(t * NS + s, last)
                    ln_epilogue(ln, ybanks, xin[:, s * 1024:(s + 1) * 1024], bxin, dst, bdst)
                issue_xt(xt, src, src_bufs, (t + 2) * NS)
            ln_flush(ln)
            P.barrier()

    def phase_swa(li, src, src_bufs, last):
        j = li // 3
        with ExitStack() as es:
            wqkv = sbt(es, [128, 8 * 1536], BF16)
            wo = sbt(es, [128, 8 * 1024], BF16)
            wb = []
            load_w_rows(wqkv[:], W["a_w_qkv"][j], 8, 1536, wb)
            load_w_rows(wo[:], W["a_w_o"][j], 8, 1024, wb)
            PERM = (0, 2, 1, 3)
            sk = sbt(es, [128, 16])
            bsk = Buf()
            P.dma("sp", sk[:], W["a_sinks"][j].partition_broadcast(128), writes=[bsk])
            P.op("act", lambda g: g.activation(sk[:], sk[:], AF.Exp), reads=[bsk], writes=[bsk])
            esk = sbt(es, [128, 16 * 128])
            esk4 = esk[:].rearrange("p (g r t) -> p g r t", g=4, r=4)
            sk4 = sk[:].rearrange("p (g r) -> p g r", g=4)
            for r_ in range(4):
                P.op("dve", lambda g: g.tensor_copy(esk4[:, :, PERM[r_], :],
                                                    sk4[:, :, r_].unsqueeze(2).broadcast_to([128, 4, 128])),
                     reads=[bsk], writes=[bsk])
            rope = sbt(es, [128, NB * 64])
            brope_l = Buf()
            P.dma("sp", rope[:], rope_d.rearrange("p j c -> p (j c)"), reads=[brope], writes=[brope_l])
            ln = make_ln(es, li, 0)
            xt = make_xt(es, 1, nbuf=4, nxt=2)
            qks = [sbt(es, [128, 1280]) for _ in range(2)]
            bqks = [Buf(), Buf()]
            t1 = sbt(es, [128, 640])
            t2 = sbt(es, [128, 640])
            bt1 = Buf()
            bt2 = Buf()
            qkr = sbt(es, [128, 1280], BF16)
            bqkr = Buf()
            QT = [sbt(es, [64, 16 * 128], BF16) for _ in range(2)]
            bQT = [Buf(), Buf()]
            KT = [sbt(es, [64, 4 * 128], BF16) for _ in range(3)]
            bKT = [Buf() for _ in range(3)]
            V = [sbt(es, [128, 256], BF16) for _ in range(4)]
            bV = [Buf() for _ in range(4)]
            pT = [sbt(es, [128, 512], BF16) for _ in range(4)]
            bpT = [Buf() for _ in range(4)]
            den = sbt(es, [64, 512])
            bden = Buf()
            OT = sbt(es, [128, 8 * 128], BF16)
            bOT = Buf()
            ones64 = ones[:, 0:64]
            P.join(ALLC, wb + [bsk, brope_l])
            npt = 0
            scale = 64.0 ** -0.5

            def front1(b):
                qk, bqk = qks[b % 2], bqks[b % 2]
                xin, bxin, xT, bxT, Wd = prep_xt(xt, src, src_bufs, b, 7)
                for n in range(3):
                    for k in range(8):
                        P.op("pe", lambda g: g.matmul(psf(n), xT[:, k * 128:(k + 1) * 128],
                                                      wqkv[:, k * 1536 + n * 512: k * 1536 + (n + 1) * 512],
                                                      start=(k == 0), stop=(k == 7)),
                             reads=[bxT], writes=[BPS[n]], sig=(k == 7))
                P.op("act", lambda g: g.copy(qk[:, 0:512], psf(0)), reads=[BPS[0]], writes=[bqk])
                P.op("act", lambda g: g.copy(qk[:, 512:1024], psf(1)), reads=[BPS[1]], writes=[bqk])
                P.op("act", lambda g: g.copy(qk[:, 1024:1280], psf(2)[:, 0:256]), reads=[BPS[2]], writes=[bqk])
                P.op("act", lambda g: g.copy(V[b % 4][:], psf(2)[:, 256:512]), reads=[BPS[2]], writes=[bV[b % 4]])
                return xin, bxin

            def front2(b):
                cur = b % 3
                qk, bqk = qks[b % 2], bqks[b % 2]
                q4 = qk[:].rearrange("p (h two d) -> p h two d", h=20, two=2)
                o4 = qkr[:].rearrange("p (h two d) -> p h two d", h=20, two=2)
                x1, x2 = q4[:, :, 0, :], q4[:, :, 1, :]
                cosb = rope[:, b * 64: b * 64 + 32].unsqueeze(1).broadcast_to([128, 20, 32])
                sinb = rope[:, b * 64 + 32: b * 64 + 64].unsqueeze(1).broadcast_to([128, 20, 32])
                t1v = t1[:].rearrange("p (h d) -> p h d", h=20)
                t2v = t2[:].rearrange("p (h d) -> p h d", h=20)
                P.op("dve", lambda g: g.tensor_tensor(t1v, x1, cosb, ALU.mult), reads=[bqk], writes=[bt1])
                P.op("pool", lambda g: g.tensor_tensor(t2v, x2, sinb, ALU.mult), reads=[bqk], writes=[bt2])
                P.op("dve", lambda g: g.tensor_tensor(o4[:, :, 0, :], t1v, t2v, ALU.subtract),
                     reads=[bt1, bt2], writes=[bqkr])
                P.op("pool", lambda g: g.tensor_tensor(t1v, x2, cosb, ALU.mult), reads=[bqk], writes=[bt1])
                P.op("dve", lambda g: g.tensor_tensor(t2v, x1, sinb, ALU.mult), reads=[bqk], writes=[bt2])
                P.op("pool", lambda g: g.tensor_tensor(o4[:, :, 1, :], t1v, t2v, ALU.add),
                     reads=[bt1, bt2], writes=[bqkr])
                for h in range(20):
                    pb = h // 8
                    sl = (h % 8) if h >= 16 else 4 * ((h % 8) // 4) + PERM[h % 4]
                    P.op("pe", lambda g: g.transpose(psh(pb)[0:64, sl * 128:(sl + 1) * 128],
                                                     qkr[:, h * 64:(h + 1) * 64], ident[:]),
                         reads=[bqkr], writes=[BPS[pb]], sig=(h % 8 == 7 or h == 19))
                QTb, bQTb = QT[b % 2], bQT[b % 2]
                P.op("act", lambda g: g.copy(QTb[:, 0:1024], psh(0)[0:64, :]), reads=[BPS[0]], writes=[bQTb])
                P.op("dve", lambda g: g.tensor_copy(QTb[:, 1024:2048], psh(1)[0:64, :]), reads=[BPS[1]], writes=[bQTb])
                P.op("act", lambda g: g.copy(KT[cur][:], psh(2)[0:64, 0:512]), reads=[BPS[2]], writes=[bKT[cur]])

            def back(b, xin, bxin):
                nonlocal npt
                cur = b % 3
                prv = (b - 1) % 3
                QTb, bQTb = QT[b % 2], bQT[b % 2]
                chunks = [(cur, b % 4, mask[:, 0:128])]
                if b > 0:
                    chunks.append((prv, (b - 1) % 4, mask[:, 128:256]))
                nchk = len(chunks)

                def emit_S(gk):
                    nonlocal npt
                    pis = []
                    for (slot, vslot, mk) in chunks:
                        pb = 3 + (npt % 2)
                        pi = npt % 4
                        npt += 1
                        P.op("pe", lambda g: g.matmul(psf(pb), KT[slot][:, gk * 128:(gk + 1) * 128],
                                                      QTb[:, gk * 512:(gk + 1) * 512], start=True, stop=True),
                             reads=[bKT[slot], bQTb], writes=[BPS[pb]], sig=True)
                        P.op("act", lambda g: g.activation(pT[pi][:], psf(pb), AF.Exp, scale=scale),
                             reads=[BPS[pb]], writes=[bpT[pi]])
                        P.op("pool", lambda g: g.tensor_tensor(
                            pT[pi][:].rearrange("p (h t) -> p h t", h=4), pT[pi][:].rearrange("p (h t) -> p h t", h=4),
                            mk.unsqueeze(1).broadcast_to([128, 4, 128]), ALU.mult),
                            reads=[bpT[pi], bm], writes=[bpT[pi]])
                        pis.append((pi, vslot))
                    return pis

                def emit_OD(gk, pis):
                    for ci, (pi, slot) in enumerate(pis):
                        P.op("pe", lambda g: g.matmul(psf(5)[0:64, :], V[slot][:, gk * 64:(gk + 1) * 64], pT[pi][:],
                                                      start=(ci == 0), stop=(ci == nchk - 1)),
                             reads=[bV[slot], bpT[pi]], writes=[BPS[5]], sig=(ci == nchk - 1))
                    for ci, (pi, slot) in enumerate(pis):
                        P.op("pe", lambda g: g.matmul(psf(6)[0:64, :], ones64, pT[pi][:],
                                                      start=(ci == 0), stop=(ci == nchk - 1)),
                             reads=[bpT[pi]], writes=[BPS[6]], sig=(ci == nchk - 1))
                    P.op("dve", lambda g: g.tensor_tensor(den[:], psf(6)[0:64, :], esk[0:64, gk * 512:(gk + 1) * 512],
                                                          ALU.add), reads=[BPS[6]], writes=[bden])
                    P.op("act", lambda g: g.activation(den[:], den[:], AF.Ln), reads=[bden], writes=[bden])
                    P.op("act", lambda g: g.activation(den[:], den[:], AF.Exp, scale=-1.0), reads=[bden], writes=[bden])
                    P.op("dve", lambda g: g.tensor_tensor(OT[0:64, gk * 256:(gk + 1) * 256], psf(5)[0:64, 0:256], den[:, 0:256],
                                                          ALU.mult), reads=[BPS[5], bden], writes=[bOT])
                    P.op("dve", lambda g: g.tensor_tensor(OT[64:128, gk * 256:(gk + 1) * 256], psf(5)[0:64, 256:512],
                                                          den[:, 256:512], ALU.mult), reads=[BPS[5], bden], writes=[bOT])

                pend = emit_S(0)
                for gk in range(4):
                    nxt = emit_S(gk + 1) if gk + 1 < 4 else None
                    emit_OD(gk, pend)
                    pend = nxt
                for n in range(2):
                    pb = 3 + n
                    for jp in range(8):
                        P.op("pe", lambda g: g.matmul(psf(pb), OT[:, jp * 128:(jp + 1) * 128],
                                                      wo[:, jp * 1024 + n * 512: jp * 1024 + (n + 1) * 512],
                                                      start=(jp == 0), stop=(jp == 7)),
                             reads=[bOT], writes=[BPS[pb]], sig=(jp == 7))
                dst, bdst = dst_for(b, last)
                ln_epilogue(ln, (3, 4), xin[:, 0:1024], bxin, dst, bdst)

            sts = {}
            sts[0] = front1(0)
            if NB > 1:
                sts[1] = front1(1)
            front2(0)
            for b in range(NB):
                if b + 2 < NB:
                    sts[b + 2] = front1(b + 2)
                if b + 1 < NB:
                    front2(b + 1)
                back(b, *sts.pop(b))
            ln_flush(ln)
            P.barrier()

    def phase_lru(li, src, src_bufs, last):
        with ExitStack() as es:
            win = sbt(es, [128, 8 * 2048], BF16)
            wr = sbt(es, [128, 8 * 256], BF16)
            wi = sbt(es, [128, 8 * 256], BF16)
            wo = sbt(es, [128, 8 * 1024], BF16)
            wb = []
            load_w_rows(win[:], W["b_w_in"][0], 8, 2048, wb)
            for blk in range(4):
                for ic in range(2):
                    c = blk * 2 + ic
                    load_cast(wr[:, c * 256:(c + 1) * 256], W["b_w_rgate"][0, blk, ic * 128:(ic + 1) * 128, :], wb)
                    load_cast(wi[:, c * 256:(c + 1) * 256], W["b_w_igate"][0, blk, ic * 128:(ic + 1) * 128, :], wb)
            load_w_rows(wo[:], W["b_w_o"][0], 8, 1024, wb)
            par = sbt(es, [128, 64])
            bpar = Buf()
            cwv = par[:, 0:32].rearrange("p (c k) -> p c k", k=4)
            for k in range(4):
                P.dma("sp", cwv[:, :, k], W["b_conv_w"][0, k].rearrange("(c p) -> p c", p=128), writes=[bpar],
                      allow_slow_non_contiguous=True)
            for nm, o_ in (("b_conv_b", 32), ("b_b_rgate", 40), ("b_b_igate", 48), ("b_lambda", 56)):
                P.dma("sp", par[:, o_:o_ + 8], W[nm][0].rearrange("(c p) -> p c", p=128), writes=[bpar],
                      allow_slow_non_contiguous=True)
            p2 = sbt(es, [128, 32])
            P.op("dve", lambda g: g.tensor_scalar(p2[:, 0:16], par[:, 40:56], 0.5, None, ALU.mult),
                 reads=[bpar], writes=[bpar])
            P.op("act", lambda g: g.activation(p2[:, 24:32], par[:, 56:64], AF.Exp, scale=-1.0), reads=[bpar], writes=[bpar])
            P.op("act", lambda g: g.activation(p2[:, 24:32], p2[:, 24:32], AF.Ln, bias=1.0, scale=1.0),
                 reads=[bpar], writes=[bpar])
            P.op("dve", lambda g: g.tensor_scalar(p2[:, 16:24], p2[:, 24:32], -4.0, None, ALU.mult),
                 reads=[bpar], writes=[bpar])
            ln = make_ln(es, li, 0)
            NS = 2
            TW = NS * 128
            xt = make_xt(es, NS)
            gT = sbt(es, [128, 8 * TW])
            bgT = Buf()
            ucT = sbt(es, [128, 8 * TW])
            bucT = Buf()
            ucb = sbt(es, [128, 8 * TW], BF16)
            bucb = Buf()
            aT = sbt(es, [128, 8 * TW])
            baT = Buf()
            iuT = sbt(es, [128, 8 * TW])
            biuT = Buf()
            sqT = sbt(es, [128, 8 * TW])
            bsqT = Buf()
            hT = sbt(es, [128, 8 * TW])
            bhT = Buf()
            yT = sbt(es, [128, 8 * TW], BF16)
            byT = Buf()
            thrs = [sbt(es, [128, TW]) for _ in range(2)]
            bthrs = [Buf(), Buf()]
            this_ = [sbt(es, [128, TW]) for _ in range(2)]
            bthis = [Buf(), Buf()]
            halo = sbt(es, [128, 8 * 3])
            bhalo = Buf()
            hprev = sbt(es, [128, 8])
            bhp = Buf()
            P.op("pool", lambda g: g.memset(halo[:], 0.0), writes=[bhalo])
            P.op("pool", lambda g: g.memset(hprev[:], 0.0), writes=[bhp])
            stg = [sbt(es, [128, TW + 3]) for _ in range(2)]
            bstg = [Buf() for _ in range(2)]
            P.join(ALLC, wb + [bpar])
            nst = 0
            nxt_prep = prep_xt(xt, src, src_bufs, 0, 0, prefetch=False)
            issue_xt(xt, src, src_bufs, NS)
            for t in range(NB // NS):
                xin, bxin, xT, bxT, Wd = nxt_prep
                for m in range(8):
                    pb = 1 + (m % 2)
                    for k in range(8):
                        P.op("pe", lambda g: g.matmul(psf(pb)[:, 0:TW], win[:, k * 2048 + m * 128: k * 2048 + (m + 1) * 128],
                                                      xT[:, k * TW:(k + 1) * TW], start=(k == 0), stop=(k == 7)),
                             reads=[bxT], writes=[BPS[pb]], sig=(k == 7))
                    P.op("act", lambda g: g.activation(gT[:, m * TW:(m + 1) * TW], psf(pb)[:, 0:TW], AF.Gelu_apprx_tanh),
                         reads=[BPS[pb]], writes=[bgT])
                for c in range(8):
                    m = 8 + c
                    pb = 1 + (c % 2)
                    si = nst % 2
                    nst += 1
                    for k in range(8):
                        P.op("pe", lambda g: g.matmul(psf(pb)[:, 0:TW], win[:, k * 2048 + m * 128: k * 2048 + (m + 1) * 128],
                                                      xT[:, k * TW:(k + 1) * TW], start=(k == 0), stop=(k == 7)),
                             reads=[bxT], writes=[BPS[pb]], sig=(k == 7))
                    sg, bsg = stg[si], bstg[si]
                    P.op("pool", lambda g: g.tensor_copy(sg[:, 0:3], halo[:, c * 3:c * 3 + 3]), reads=[bhalo], writes=[bsg])
                    P.op("act", lambda g: g.copy(sg[:, 3:TW + 3], psf(pb)[:, 0:TW]), reads=[BPS[pb]], writes=[bsg])
                    P.op("pool", lambda g: g.tensor_copy(halo[:, c * 3:c * 3 + 3], sg[:, TW:TW + 3]),
                         reads=[bsg], writes=[bhalo])
                    uc = ucT[:, c * TW:(c + 1) * TW]
                    P.op("act", lambda g: g.activation(uc, sg[:, 3:TW + 3], AF.Identity, bias=par[:, 32 + c:33 + c],
                                                       scale=par[:, c * 4 + 3:c * 4 + 4]), reads=[bsg], writes=[bucT])
                    for k in range(3):
                        P.op("dve", lambda g: g.scalar_tensor_tensor(uc, sg[:, k:TW + k], par[:, c * 4 + k:c * 4 + k + 1],
                                                                     uc, ALU.mult, ALU.add),
                             reads=[bsg, bucT], writes=[bucT])
                    P.op("pool", lambda g: g.tensor_copy(ucb[:, c * TW:(c + 1) * TW], uc), reads=[bucT], writes=[bucb])
                if t + 1 < NB // NS:
                    nxt_prep = prep_xt(xt, src, src_bufs, (t + 1) * NS, 0, prefetch=False)
                for blk in range(4):
                    for jc in range(2):
                        o_ = blk * 2 + jc
                        thr, bthr = thrs[o_ % 2], bthrs[o_ % 2]
                        thi, bthi = this_[o_ % 2], bthis[o_ % 2]
                        gb = (3, 4) if o_ % 2 == 0 else (5, 2)
                        for gi, (wg, pbg) in enumerate(((wr, gb[0]), (wi, gb[1]))):
                            for ic in range(2):
                                c = blk * 2 + ic
                                P.op("pe", lambda g: g.matmul(psf(pbg)[:, 0:TW], wg[:, c * 256 + jc * 128: c * 256 + (jc + 1) * 128],
                                                              ucb[:, c * TW:(c + 1) * TW], start=(ic == 0), stop=(ic == 1)),
                                     reads=[bucb], writes=[BPS[pbg]], sig=(ic == 1))
                        P.op("act", lambda g: g.activation(thr[:], psf(gb[0])[:, 0:TW], AF.Tanh, bias=p2[:, o_:o_ + 1], scale=0.5),
                             reads=[BPS[gb[0]]], writes=[bthr])
                        P.op("act", lambda g: g.activation(aT[:, o_ * TW:(o_ + 1) * TW], thr[:], AF.Exp,
                                                           bias=p2[:, 16 + o_:17 + o_], scale=p2[:, 16 + o_:17 + o_]),
                             reads=[bthr], writes=[baT])
                        P.op("act", lambda g: g.activation(thi[:], psf(gb[1])[:, 0:TW], AF.Tanh, bias=p2[:, 8 + o_:9 + o_], scale=0.5),
                             reads=[BPS[gb[1]]], writes=[bthi])
                        P.op("dve", lambda g: g.scalar_tensor_tensor(iuT[:, o_ * TW:(o_ + 1) * TW], thi[:], 1.0,
                                                                     ucT[:, o_ * TW:(o_ + 1) * TW], ALU.add, ALU.mult),
                             reads=[bthi, bucT], writes=[biuT])
                P.op("pool", lambda g: g.tensor_tensor(sqT[:], aT[:], aT[:], ALU.mult), reads=[baT], writes=[bsqT])
                P.op("act", lambda g: g.activation(sqT[:], sqT[:], AF.Ln, bias=1.0, scale=-1.0), reads=[bsqT], writes=[bsqT])
                P.op("act", lambda g: g.activation(sqT[:], sqT[:], AF.Exp, scale=0.5), reads=[bsqT], writes=[bsqT])
                P.op("dve", lambda g: g.scalar_tensor_tensor(sqT[:], sqT[:], 0.5, iuT[:], ALU.mult, ALU.mult),
                     reads=[bsqT, biuT], writes=[bsqT])
                for o_ in range(8):
                    P.op("dve", lambda g: g.tensor_tensor_scan(hT[:, o_ * TW:(o_ + 1) * TW], aT[:, o_ * TW:(o_ + 1) * TW],
                                                               sqT[:, o_ * TW:(o_ + 1) * TW], hprev[:, o_:o_ + 1],
                                                               ALU.mult, ALU.add),
                         reads=[baT, bsqT, bhp], writes=[bhT])
                P.op("dve", lambda g: g.tensor_copy(hprev[:], hT[:].rearrange("p (c t) -> p c t", c=8)[:, :, TW - 1]),
                     reads=[bhT], writes=[bhp])
                P.op("pool", lambda g: g.tensor_tensor(yT[:], hT[:], gT[:], ALU.mult), reads=[bhT, bgT], writes=[byT])
                for s in range(NS):
                    for n in range(2):
                        pb = 6 + n
                        for k in range(8):
                            P.op("pe", lambda g: g.matmul(psf(pb), yT[:, k * TW + s * 128: k * TW + (s + 1) * 128],
                                                          wo[:, k * 1024 + n * 512: k * 1024 + (n + 1) * 512],
                                                          start=(k == 0), stop=(k == 7)),
                                 reads=[byT], writes=[BPS[pb]], sig=(k == 7))
                    dst, bdst = dst_for(t * NS + s, last)
                    ln_epilogue(ln, (6, 7), xin[:, s * 1024:(s + 1) * 1024], bxin, dst, bdst)
                issue_xt(xt, src, src_bufs, (t + 2) * NS)
            ln_flush(ln)
            P.barrier()

    def phase_mla(li, src, src_bufs, last):
        NT = T // 512
        with ExitStack() as es:
            wdn = sbt(es, [128, 8 * 704], BF16)
            wuqn = sbt(es, [128, 3 * 1024], BF16)
            wuqr = sbt(es, [128, 3 * 512], BF16)
            wukk = sbt(es, [128, 2 * 1024], BF16)
            wukv = sbt(es, [128, 2 * 1024], BF16)
            wo = sbt(es, [128, 8 * 1024], BF16)
            wb = []
            load_w_rows(wdn[:], W["c_w_down"][0], 8, 704, wb)
            for kc in range(3):
                src3 = W["c_w_uq"][0, kc * 128:(kc + 1) * 128, :].rearrange("p (h d) -> p h d", h=8)
                load_cast(wuqn[:, kc * 1024:(kc + 1) * 1024].rearrange("p (h d) -> p h d", h=8), src3[:, :, 0:128], wb)
                load_cast(wuqr[:, kc * 512:(kc + 1) * 512].rearrange("p (h d) -> p h d", h=8), src3[:, :, 128:192], wb)
            for kc in range(2):
                src3 = W["c_w_ukv"][0, kc * 128:(kc + 1) * 128, :].rearrange("p (h d) -> p h d", h=8)
                load_cast(wukk[:, kc * 1024:(kc + 1) * 1024].rearrange("p (h d) -> p h d", h=8), src3[:, :, 0:128], wb)
                load_cast(wukv[:, kc * 1024:(kc + 1) * 1024].rearrange("p (h d) -> p h d", h=8), src3[:, :, 128:256], wb)
            load_w_rows(wo[:], W["c_w_o"][0], 8, 1024, wb)
            gn = sbt(es, [128, 640])
            bgn = Buf()
            P.dma("sp", gn[:, 0:384], W["c_q_norm"][0].partition_broadcast(128), writes=[bgn])
            P.dma("sp", gn[:, 384:640], W["c_kv_norm"][0].partition_broadcast(128), writes=[bgn])
            epsr = sbt(es, [128, 1])
            P.op("pool", lambda g: g.memset(epsr[:], RMS_EPS), writes=[bgn])
            rope = sbt(es, [128, NB * 64])
            brl = Buf()
            P.dma("sp", rope[:], rope_d.rearrange("p j c -> p (j c)"), reads=[brope], writes=[brl])
            ckvT = sbt(es, [128, 2 * T], BF16)
            bckvT = Buf()
            krT = sbt(es, [64, T], BF16)
            bkrT = Buf()
            P.join(ALLC, wb + [bgn, brl])
            scale = 192.0 ** -0.5
            with ExitStack() as esA:
                xt = make_xt(esA, 4)
                junk = sbt(esA, [128, 384])
                bjunk = Buf()
                ss = sbt(esA, [128, 4])
                bss = Buf()
                lat = sbt(esA, [128, 704], BF16)
                blat = Buf()
                tr = [sbt(esA, [128, 32]) for _ in range(2)]
                btr = [Buf(), Buf()]
                cqT = sbt(esA, [128, 3 * 512], BF16)
                bcqT = Buf()
                qrf = sbt(esA, [128, 512])
                bqrf = Buf()
                t1 = sbt(esA, [128, 256])
                t2 = sbt(esA, [128, 256])
                bt1 = Buf()
                bt2 = Buf()
                qrb = sbt(esA, [128, 512], BF16)
                bqrb = Buf()
                qrT = sbt(esA, [64, 8 * 512], BF16)
                bqrT = Buf()
                Vt = sbt(esA, [128, 4 * 1024], BF16)
                bVt = Buf()
                bscr = Buf()
                for t in range(NT):
                    xin, bxin, xT, bxT, Wd = prep_xt(xt, src, src_bufs, t * 4, 0)
                    for s in range(4):
                        blk = t * 4 + s
                        tok = slice(s * 128, (s + 1) * 128)
                        for (pb, c0, cn) in ((1, 0, 384), (2, 384, 320)):
                            for k in range(8):
                                P.op("pe", lambda g: g.matmul(psf(pb)[:, 0:cn], xT[:, k * 512 + s * 128: k * 512 + (s + 1) * 128],
                                                              wdn[:, k * 704 + c0: k * 704 + c0 + cn],
                                                              start=(k == 0), stop=(k == 7)),
                                     reads=[bxT], writes=[BPS[pb]], sig=(k == 7))
                        P.op("act", lambda g: g.activation(junk[:, 0:384], psf(1)[:, 0:384], AF.Square, accum_out=ss[:, 0:1]),
                             reads=[BPS[1]], writes=[bjunk, bss])
                        P.op("act", lambda g: g.activation(junk[:, 0:256], psf(2)[:, 0:256], AF.Square, accum_out=ss[:, 1:2]),
                             reads=[BPS[2]], writes=[bjunk, bss])
                        P.op("act", lambda g: g.activation(ss[:, 2:3], ss[:, 0:1], AF.Ln, bias=epsr[:, 0:1], scale=1.0 / 384.0),
                             reads=[bss], writes=[bss])
                        P.op("act", lambda g: g.activation(ss[:, 3:4], ss[:, 1:2], AF.Ln, bias=epsr[:, 0:1], scale=1.0 / 256.0),
                             reads=[bss], writes=[bss])
                        P.op("act", lambda g: g.activation(ss[:, 2:4], ss[:, 2:4], AF.Exp, scale=-0.5), reads=[bss], writes=[bss])
                        P.op("dve", lambda g: g.scalar_tensor_tensor(lat[:, 0:384], psf(1)[:, 0:384], ss[:, 2:3], gn[:, 0:384],
                                                                     ALU.mult, ALU.mult),
                             reads=[BPS[1], bss], writes=[blat])
                        P.op("dve", lambda g: g.scalar_tensor_tensor(lat[:, 384:640], psf(2)[:, 0:256], ss[:, 3:4], gn[:, 384:640],
                                                                     ALU.mult, ALU.mult),
                             reads=[BPS[2], bss], writes=[blat])
                        cs = rope[:, blk * 64: blk * 64 + 32]
                        sn = rope[:, blk * 64 + 32: blk * 64 + 64]
                        k1 = psf(2)[:, 256:288]
                        k2 = psf(2)[:, 288:320]
                        P.op("dve", lambda g: g.tensor_tensor(tr[0][:], k1, cs, ALU.mult), reads=[BPS[2]], writes=[btr[0]])
                        P.op("dve", lambda g: g.tensor_tensor(tr[1][:], k2, sn, ALU.mult), reads=[BPS[2]], writes=[btr[1]])
                        P.op("dve", lambda g: g.tensor_tensor(lat[:, 640:672], tr[0][:], tr[1][:], ALU.subtract),
                             reads=[btr[0], btr[1]], writes=[blat])
                        P.op("dve", lambda g: g.tensor_tensor(tr[0][:], k2, cs, ALU.mult), reads=[BPS[2]], writes=[btr[0]])
                        P.op("dve", lambda g: g.tensor_tensor(tr[1][:], k1, sn, ALU.mult), reads=[BPS[2]], writes=[btr[1]])
                        P.op("dve", lambda g: g.tensor_tensor(lat[:, 672:704], tr[0][:], tr[1][:], ALU.add),
                             reads=[btr[0], btr[1]], writes=[blat])
                        for c in range(5):
                            P.op("pe", lambda g: g.transpose(psh(3)[:, c * 128:(c + 1) * 128], lat[:, c * 128:(c + 1) * 128], ident[:]),
                                 reads=[blat], writes=[BPS[3]], sig=False)
                        P.op("pe", lambda g: g.transpose(psh(3)[0:64, 640:768], lat[:, 640:704], ident[:]),
                             reads=[blat], writes=[BPS[3]])
                        P.op("act", lambda g: g.copy(cqT[:].rearrange("p (c t) -> p c t", c=3)[:, :, tok],
                                                     psh(3)[:, 0:384].rearrange("p (c t) -> p c t", c=3)),
                             reads=[BPS[3]], writes=[bcqT])
                        P.op("act", lambda g: g.copy(
                            ckvT[:].rearrange("p (c t) -> p c t", c=2)[:, :, blk * 128:(blk + 1) * 128],
                            psh(3)[:, 384:640].rearrange("p (c t) -> p c t", c=2)), reads=[BPS[3]], writes=[bckvT])
                        P.op("act", lambda g: g.copy(krT[:, blk * 128:(blk + 1) * 128], psh(3)[0:64, 640:768]),
                             reads=[BPS[3]], writes=[bkrT])
                        for kc in range(3):
                            P.op("pe", lambda g: g.matmul(psf(4), cqT[:, kc * 512 + s * 128: kc * 512 + (s + 1) * 128],
                                                          wuqr[:, kc * 512:(kc + 1) * 512], start=(kc == 0), stop=(kc == 2)),
                                 reads=[bcqT], writes=[BPS[4]], sig=(kc == 2))
                        P.op("act", lambda g: g.copy(qrf[:], psf(4)), reads=[BPS[4]], writes=[bqrf])
                        q4 = qrf[:].rearrange("p (h two d) -> p h two d", h=8, two=2)
                        o4 = qrb[:].rearrange("p (h two d) -> p h two d", h=8, two=2)
                        x1, x2 = q4[:, :, 0, :], q4[:, :, 1, :]
                        cosb = cs.unsqueeze(1).broadcast_to([128, 8, 32])
                        sinb = sn.unsqueeze(1).broadcast_to([128, 8, 32])
                        t1v = t1[:].rearrange("p (h d) -> p h d", h=8)
                        t2v = t2[:].rearrange("p (h d) -> p h d", h=8)
                        P.op("pool", lambda g: g.tensor_tensor(t1v, x1, cosb, ALU.mult), reads=[bqrf], writes=[bt1])
                        P.op("pool", lambda g: g.tensor_tensor(t2v, x2, sinb, ALU.mult), reads=[bqrf], writes=[bt2])
                        P.op("pool", lambda g: g.tensor_tensor(o4[:, :, 0, :], t1v, t2v, ALU.subtract),
                             reads=[bt1, bt2], writes=[bqrb])
                        P.op("pool", lambda g: g.tensor_tensor(t1v, x2, cosb, ALU.mult), reads=[bqrf], writes=[bt1])
                        P.op("pool", lambda g: g.tensor_tensor(t2v, x1, sinb, ALU.mult), reads=[bqrf], writes=[bt2])
                        P.op("pool", lambda g: g.tensor_tensor(o4[:, :, 1, :], t1v, t2v, ALU.add),
                             reads=[bt1, bt2], writes=[bqrb])
                        for h in range(8):
                            P.op("pe", lambda g: g.transpose(psh(5)[0:64, h * 128:(h + 1) * 128], qrb[:, h * 64:(h + 1) * 64], ident[:]),
                                 reads=[bqrb], writes=[BPS[5]], sig=(h == 7))
                        P.op("act", lambda g: g.copy(qrT[:].rearrange("p (h t) -> p h t", h=8)[:, :, tok],
                                                     psh(5)[0:64, :].rearrange("p (h t) -> p h t", h=8)),
                             reads=[BPS[5]], writes=[bqrT])
                        for n in range(2):
                            pb = 6 + n
                            for kc in range(2):
                                P.op("pe", lambda g: g.matmul(
                                    psf(pb), ckvT[:, kc * T + blk * 128: kc * T + (blk + 1) * 128],
                                    wukv[:, kc * 1024 + n * 512: kc * 1024 + (n + 1) * 512], start=(kc == 0), stop=(kc == 1)),
                                    reads=[bckvT], writes=[BPS[pb]], sig=(kc == 1))
                            cast("dve" if n else "act", Vt[:, s * 1024 + n * 512: s * 1024 + (n + 1) * 512], psf(pb),
                                 [BPS[pb]], [bVt])
                    tk = slice(t * 512, (t + 1) * 512)
                    P.dma("sp", cqnT_d[:, :, tk].rearrange("c p t -> p c t"), cqT[:].rearrange("p (c t) -> p c t", c=3),
                          reads=[bcqT], writes=[bscr])
                    P.dma("sp", qrT_d[:, :, tk].rearrange("h p t -> p h t"), qrT[:].rearrange("p (h t) -> p h t", h=8),
                          reads=[bqrT], writes=[bscr])
                    P.dma("sp", v_d[tk, :].rearrange("(s p) f -> p s f", p=128), Vt[:].rearrange("p (s f) -> p s f", s=4),
                          reads=[bVt], writes=[bscr])
                P.barrier()
            with ExitStack() as esB:
                KTh = sbt(esB, [128, T], BF16)
                bKTh = Buf()
                Vh = sbt(esB, [128, T], BF16)
                bVh = Buf()
                cq = [sbt(esB, [128, 3 * 512], BF16) for _ in range(2)]
                bcq = [Buf(), Buf()]
                qr = [sbt(esB, [64, 512], BF16) for _ in range(2)]
                bqr = [Buf(), Buf()]
                QnT = sbt(esB, [128, 512], BF16)
                bQn = Buf()
                pT = [sbt(esB, [128, 512], BF16) for _ in range(6)]
                bpT = [Buf() for _ in range(6)]
                SBK = (3, 4, 5, 0, 1)
                rden = sbt(esB, [128, 512])
                brd = Buf()
                dacc = sbt(esB, [128, 512])
                bdacc = Buf()
                ones_f = sbt(esB, [128, 128])
                bof = Buf()
                P.op("pool", lambda g: g.memset(ones_f[:], 1.0), writes=[bof])
                oT = [sbt(esB, [128, 512], BF16) for _ in range(2)]
                boT = [Buf(), Buf()]
                batt = Buf()
                npt = 0
                nq = 0
                for h in range(8):
                    for c0 in range(0, NB, 16):
                        cn_ = min(16, NB - c0)
                        P.dma("sp", Vh[:, c0 * 128:(c0 + cn_) * 128].rearrange("p (c d) -> p c d", d=128),
                              v_d[c0 * 128:(c0 + cn_) * 128, h * 128:(h + 1) * 128].rearrange("(c p) d -> p c d", p=128),
                              reads=[bscr], writes=[bVh])
                    for tc in range(NT):
                        pb = tc % 2
                        for kc in range(2):
                            P.op("pe", lambda g: g.matmul(psf(pb), wukk[:, kc * 1024 + h * 128: kc * 1024 + (h + 1) * 128],
                                                          ckvT[:, kc * T + tc * 512: kc * T + (tc + 1) * 512],
                                                          start=(kc == 0), stop=(kc == 1)),
                                 reads=[bckvT], writes=[BPS[pb]], sig=(kc == 1))
                        cast("dve" if tc % 2 else "act", KTh[:, tc * 512:(tc + 1) * 512], psf(pb), [BPS[pb]], [bKTh])
                    for jq in range(NT):
                        qi = nq % 2
                        nq += 1
                        tk = slice(jq * 512, (jq + 1) * 512)
                        P.dma("sp", cq[qi][:].rearrange("p (c t) -> p c t", c=3), cqnT_d[:, :, tk].rearrange("c p t -> p c t"),
                              reads=[bscr], writes=[bcq[qi]])
                        P.dma("sp", qr[qi][:], qrT_d[h, :, tk], reads=[bscr], writes=[bqr[qi]])
                        for kc in range(3):
                            P.op("pe", lambda g: g.matmul(psf(2), wuqn[:, kc * 1024 + h * 128: kc * 1024 + (h + 1) * 128],
                                                          cq[qi][:, kc * 512:(kc + 1) * 512], start=(kc == 0), stop=(kc == 2)),
                                 reads=[bcq[qi]], writes=[BPS[2]], sig=(kc == 2))
                        cast("dve", QnT[:], psf(2), [BPS[2]], [bQn])
                        nch = 4 * (jq + 1)

                        def emit_S(c):
                            nonlocal npt
                            dg = c - 4 * jq
                            q0 = max(0, dg) * 128
                            pb = SBK[npt % 5]
                            pi = npt % 6
                            npt += 1
                            P.op("pe", lambda g: g.matmul(psf(pb)[:, q0:512], KTh[:, c * 128:(c + 1) * 128], QnT[:, q0:512],
                                                          start=True, stop=False),
                                 reads=[bKTh, bQn], writes=[BPS[pb]], sig=False)
                            P.op("pe", lambda g: g.matmul(psf(pb)[:, q0:512], krT[:, c * 128:(c + 1) * 128], qr[qi][:, q0:512],
                                                          start=False, stop=True),
                                 reads=[bkrT, bqr[qi]], writes=[BPS[pb]], sig=True)
                            P.op("act", lambda g: g.activation(pT[pi][:, q0:512], psf(pb)[:, q0:512], AF.Exp, scale=scale),
                                 reads=[BPS[pb]], writes=[bpT[pi]])
                            if dg >= 0:
                                P.op("pool", lambda g: g.tensor_tensor(pT[pi][:, q0:q0 + 128], pT[pi][:, q0:q0 + 128],
                                                                       mask[:, 0:128], ALU.mult),
                                     reads=[bpT[pi], bm], writes=[bpT[pi]])
                            return (c, pi, q0)

                        def emit_OD(c, pi, q0):
                            P.op("pe", lambda g: g.matmul(psf(6)[:, q0:512], Vh[:, c * 128:(c + 1) * 128], pT[pi][:, q0:512],
                                                          start=(c == 0), stop=(c == nch - 1)),
                                 reads=[bVh, bpT[pi]], writes=[BPS[6]], sig=(c == nch - 1))
                            P.op("pe", lambda g: g.matmul(psf(7)[:, q0:512], ones[:], pT[pi][:, q0:512],
                                                          start=(c == 0), stop=(c == nch - 1)),
                                 reads=[bpT[pi]], writes=[BPS[7]], sig=(c == nch - 1))

                        pend = []
                        for c in range(nch):
                            pend.append(emit_S(c))
                            if len(pend) > 3:
                                emit_OD(*pend.pop(0))
                        while pend:
                            emit_OD(*pend.pop(0))
                        P.op("act", lambda g: g.activation(rden[:], psf(7), AF.Ln), reads=[BPS[7]], writes=[brd])
                        P.op("act", lambda g: g.activation(rden[:], rden[:], AF.Exp, scale=-1.0), reads=[brd], writes=[brd])
                        P.op("dve", lambda g: g.tensor_tensor(oT[qi][:], psf(6), rden[:], ALU.mult),
                             reads=[BPS[6], brd], writes=[boT[qi]])
                        P.dma("sp", attT_d[h, :, tk], oT[qi][:], reads=[boT[qi]], writes=[batt])
                P.barrier()
            with ExitStack() as esC:
                ln = make_ln(esC, li, 0)
                aT = [sbt(esC, [128, 8 * 512], BF16) for _ in range(2)]
                baT = [Buf(), Buf()]
                xin = [sbt(esC, [128, 4 * 1024]) for _ in range(2)]
                bxin = [Buf(), Buf()]
                def loadC(t):
                    i = t % 2
                    tk = slice(t * 512, (t + 1) * 512)
                    P.dma("sp", aT[i][:].rearrange("p (h t) -> p h t", h=8), attT_d[:, :, tk].rearrange("h p t -> p h t"),
                          reads=[batt], writes=[baT[i]])
                    P.dma("sp", xin[i][:].rearrange("p (s f) -> p s f", s=4),
                          src[t * 512:(t + 1) * 512, :].rearrange("(s p) f -> p s f", p=128),
                          reads=[src_bufs[t * 4 + s] for s in range(4)] if src_bufs else [], writes=[bxin[i]])

                loadC(0)
                for t in range(NT):
                    i = t % 2
                    for s in range(4):
                        for n in range(2):
                            pb = (s % 2) * 2 + n
                            for hh in range(8):
                                P.op("pe", lambda g: g.matmul(psf(pb), aT[i][:, hh * 512 + s * 128: hh * 512 + (s + 1) * 128],
                                                              wo[:, hh * 1024 + n * 512: hh * 1024 + (n + 1) * 512],
                                                              start=(hh == 0), stop=(hh == 7)),
                                     reads=[baT[i]], writes=[BPS[pb]], sig=(hh == 7))
                        dst, bdst = dst_for(t * 4 + s, last)
                        ln_epilogue(ln, ((s % 2) * 2, (s % 2) * 2 + 1), xin[i][:, s * 1024:(s + 1) * 1024], bxin[i], dst, bdst)
                        if s == 0 and t + 1 < NT:
                            loadC(t + 1)
                ln_flush(ln)
                P.barrier()

    subs = []
    for li in range(DEPTH):
        subs.append((("swa", "lru", "mla")[li % 3], li))
        subs.append(("xattn", li))
        subs.append(("ffn", li))
    subs = subs[start:upto]
    for si, (kind, li) in enumerate(subs):
        first = si == 0
        last = si == len(subs) - 1
        src = x_in if first else xs_d
        sb_ = None if first else xs_b
        {"swa": phase_swa, "lru": phase_lru, "mla": phase_mla, "xattn": phase_xattn, "ffn": phase_ffn}[kind](li, src, sb_, last)
    P.barrier(["sp"])
    ges.close()
    return nc, P.nins


def make_consts(T):
    NB = T // 128
    k = np.arange(128)[:, None]
    q = np.arange(128)[None, :]
    mask = np.concatenate([(k <= q), (k > q)], axis=1).astype(np.float32)
    pos = (np.arange(NB)[None, :] * 128 + np.arange(128)[:, None]).astype(np.float32)
    iota = np.tile(np.arange(32, dtype=np.float32)[None, :], (128, 1))
    return {"c_ident": np.eye(128, dtype=np.float32), "c_mask": mask, "c_pos": pos, "c_iota": iota}


_CACHE = {}


def run(inputs, T, n_seq, upto=12, start=0):
    key = (T, upto, start)
    if key not in _CACHE:
        _CACHE[key] = build(T, upto, start)[0]
    nc = _CACHE[key]
    consts = make_consts(T)
    wts = {n: np.ascontiguousarray(np.asarray(inputs[n], dtype=np.float32)) for n, _ in WSHAPES}
    x = np.asarray(inputs["x"], dtype=np.float32)
    mem = np.asarray(inputs["mem"], dtype=np.float32)
    in_maps = []
    for c in range(N_CORES):
        b = c % n_seq
        m = {"x": np.ascontiguousarray(x[b]), "mem": np.ascontiguousarray(mem[b])}
        m.update(wts)
        m.update(consts)
        in_maps.append(m)
    res = run_bass_kernel_spmd(nc, in_maps, core_ids=list(range(N_CORES)))
    return np.stack([np.asarray(res.results[b]["out"]) for b in range(n_seq)], axis=0).astype(np.float32)


def kernel(**inputs):
    x = inputs["x"]
    B, S, _ = x.shape
    return run(inputs, S, B)
```

```python
import math
import numpy as np
from contextlib import ExitStack
import concourse.bass as bass
import concourse.mybir as mybir
from concourse.bass_utils import run_bass_kernel_spmd

F32 = mybir.dt.float32
BF16 = mybir.dt.bfloat16
I32 = mybir.dt.int32
AF = mybir.ActivationFunctionType
ALU = mybir.AluOpType

D = 1024
DEPTH = 4
MEM = 256
DFF = 2816
ALPHA = 8.0 ** 0.25
LN_EPS = 1e-5
RMS_EPS = 1e-6
N_CORES = 8

WSHAPES = [
    ("a_w_qkv", (2, 1024, 1536)), ("a_sinks", (2, 16)), ("a_w_o", (2, 1024, 1024)),
    ("b_w_in", (1, 1024, 2048)), ("b_conv_w", (1, 4, 1024)), ("b_conv_b", (1, 1024)),
    ("b_w_rgate", (1, 4, 256, 256)), ("b_b_rgate", (1, 1024)), ("b_w_igate", (1, 4, 256, 256)),
    ("b_b_igate", (1, 1024)), ("b_lambda", (1, 1024)), ("b_w_o", (1, 1024, 1024)),
    ("c_w_down", (1, 1024, 704)), ("c_q_norm", (1, 384)), ("c_kv_norm", (1, 256)),
    ("c_w_uq", (1, 384, 1536)), ("c_w_ukv", (1, 256, 2048)), ("c_w_o", (1, 1024, 1024)),
    ("mem_w_kv", (1024, 2048)), ("x_w_q", (4, 1024, 1024)), ("x_w_o", (4, 1024, 1024)),
    ("f_w_up", (4, 1024, 5632)), ("f_conv_w", (4, 3, 5632)), ("f_conv_b", (4, 5632)),
    ("f_w_down", (4, 2816, 1024)), ("ln_g", (4, 3, 1024)), ("ln_b", (4, 3, 1024)),
]


class Buf:
    __slots__ = ("w", "r")

    def __init__(self):
        self.w = None
        self.r = {}


class Prog:
    CE = ("pe", "act", "dve", "pool")
    NDMA = 8

    def __init__(self, nc, es):
        self.nc = nc
        self.es = es
        self.eng = dict(pe=nc.tensor, act=nc.scalar, dve=nc.vector, pool=nc.gpsimd, sp=nc.sync)
        self.sems = {}
        self.cnt = {}
        for e in self.CE:
            self.sems[e] = es.enter_context(nc.semaphore("s_" + e))
            self.cnt[e] = 0
        self.dq = {}
        self.waited = {e: {} for e in self.eng}
        self.nins = 0

    def _dq(self, q):
        if q not in self.dq:
            keys = []
            for i in range(self.NDMA):
                k = "d_%s_%d" % (q, i)
                self.sems[k] = self.es.enter_context(self.nc.semaphore(k))
                keys.append(k)
            self.dq[q] = [keys, 0]
        return self.dq[q]

    def _wait(self, e, key, val):
        if self.waited[e].get(key, 0) >= val:
            return
        self.eng[e].wait_ge(self.sems[key], val)
        self.waited[e][key] = val
        self.nins += 1

    def _deps(self, e, mykey, reads, writes):
        deps = {}
        for b in reads:
            if b.w is not None:
                k, v = b.w
                if not (k == mykey and e == "pe"):
                    if deps.get(k, 0) < v:
                        deps[k] = v
        same_ok = (e == "pe") or (e not in self.CE)
        for b in writes:
            if b.w is not None:
                k, v = b.w
                if not (k == mykey and same_ok) and deps.get(k, 0) < v:
                    deps[k] = v
            for k, v in b.r.items():
                if not (k == mykey and same_ok) and deps.get(k, 0) < v:
                    deps[k] = v
        for k, v in deps.items():
            self._wait(e, k, v)

    def _mark(self, key, val, reads, writes):
        for b in writes:
            b.w = (key, val)
            b.r = {}
        for b in reads:
            if b.r.get(key, 0) < val:
                b.r[key] = val

    def op(self, e, fn, reads=(), writes=(), sig=True):
        self._deps(e, e, reads, writes)
        ins = fn(self.eng[e])
        if sig:
            self.cnt[e] += 1
            ins.then_inc(self.sems[e], 1)
            val = self.cnt[e]
        else:
            val = self.cnt[e] + 1
        self._mark(e, val, reads, writes)
        self.nins += 1
        return ins

    def dma(self, q, out, in_, reads=(), writes=(), **kw):
        keys, n = self._dq(q)
        key = keys[n % self.NDMA]
        rnd = n // self.NDMA
        if rnd > 0:
            self._wait(q, key, 16 * rnd)
        self._deps(q, key, reads, writes)
        ins = self.eng[q].dma_start(out=out, in_=in_, **kw)
        ins.then_inc(self.sems[key], 16)
        self.dq[q][1] = n + 1
        self._mark(key, 16 * (rnd + 1), reads, writes)
        self.nins += 1
        return ins

    def totals(self):
        tot = {e: self.cnt[e] for e in self.CE}
        for q, (keys, n) in self.dq.items():
            for i, k in enumerate(keys):
                c = (n - i + self.NDMA - 1) // self.NDMA
                if c > 0:
                    tot[k] = 16 * c
        return tot

    def barrier(self, engines=None):
        tot = self.totals()
        for e in (engines or list(self.eng)):
            for k, v in tot.items():
                if v > 0 and k != e:
                    self._wait(e, k, v)

    def join(self, engines, bufs):
        for e in engines:
            for b in bufs:
                if b.w is not None and b.w[0] != e:
                    self._wait(e, b.w[0], b.w[1])


def build(T, upto=12, start=0):
    NB = T // 128
    assert T % 512 == 0
    nc = bass.Bass("TRN2", target_bir_lowering=False)

    def din(name, shape):
        return nc.dram_tensor(name, list(shape), F32, kind="ExternalInput").ap()

    x_in = din("x", [T, D])
    mem_in = din("mem", [MEM, D])
    W = {n: din(n, s) for n, s in WSHAPES}
    c_ident = din("c_ident", [128, 128])
    c_mask = din("c_mask", [128, 256])
    c_pos = din("c_pos", [128, NB])
    c_iota = din("c_iota", [128, 32])
    out_d = nc.dram_tensor("out", [T, D], F32, kind="ExternalOutput").ap()
    xs_d = nc.dram_tensor("xs_scr", [T, D], F32).ap()
    rope_d = nc.dram_tensor("rope_scr", [128, NB, 64], F32).ap()
    cqnT_d = nc.dram_tensor("cqnT_scr", [3, 128, T], BF16).ap()
    qrT_d = nc.dram_tensor("qrT_scr", [8, 64, T], BF16).ap()
    v_d = nc.dram_tensor("v_scr", [T, 1024], BF16).ap()
    attT_d = nc.dram_tensor("attT_scr", [8, 128, T], BF16).ap()
    memKT_d = nc.dram_tensor("memKT_scr", [128, 2048], BF16).ap()
    memV_d = nc.dram_tensor("memV_scr", [128, 2048], BF16).ap()

    ges = ExitStack()
    P = Prog(nc, ges)
    uid = [0]

    def sbt(es, shape, dt=F32):
        uid[0] += 1
        return es.enter_context(nc.sbuf_tensor("t%d" % uid[0], list(shape), dt))

    PSB = [ges.enter_context(nc.psum_tensor("psb%d" % i, [128, 512], F32)) for i in range(8)]
    BPS = [Buf() for _ in range(8)]

    def psf(i):
        return PSB[i][:]

    def psh(i):
        return PSB[i][:].bitcast(BF16)

    ident_f = sbt(ges, [128, 128])
    ident = sbt(ges, [128, 128], BF16)
    mask_f = sbt(ges, [128, 256])
    mask = sbt(ges, [128, 256], BF16)
    ones = sbt(ges, [128, 128], BF16)
    wstg = [sbt(ges, [128, 1024]) for _ in range(3)]
    bwstg = [Buf() for _ in range(3)]
    wcnt = [0]
    xs_b = [Buf() for _ in range(NB)]
    out_b = [Buf() for _ in range(NB)]

    def cast(e, out, in_, reads, writes):
        if e == "act":
            P.op("act", lambda g: g.copy(out, in_), reads=reads, writes=writes)
        else:
            P.op(e, lambda g: g.tensor_copy(out, in_), reads=reads, writes=writes)

    CAST_ENG = ("act", "dve", "act")

    def load_cast(dst, src, wb):
        shape = list(dst.shape)
        npart = shape[0]
        n = 1
        for s_ in shape[1:]:
            n *= s_
        assert n <= 1024, shape
        i = wcnt[0] % 3
        wcnt[0] += 1
        view = wstg[i][0:npart, 0:n]
        if len(shape) == 3:
            view = view.rearrange("p (a b) -> p a b", a=shape[1])
        P.dma("sp", view, src, writes=[bwstg[i]])
        b = Buf()
        cast(CAST_ENG[i], dst, view, [bwstg[i]], [b])
        wb.append(b)

    def load_w_rows(dst2d, src2d, nchunk, ncols, wb, row0=0):
        for c in range(nchunk):
            for j0 in range(0, ncols, 1024):
                w_ = min(1024, ncols - j0)
                load_cast(dst2d[:, c * ncols + j0: c * ncols + j0 + w_],
                          src2d[row0 + c * 128: row0 + (c + 1) * 128, j0:j0 + w_], wb)

    ALLC = ("pe", "act", "dve", "pool")

    with ExitStack() as es:
        b0 = Buf()
        P.dma("sp", ident_f[:], c_ident[:, :], writes=[b0])
        bi = Buf()
        cast("dve", ident[:], ident_f[:], [b0], [bi])
        b1 = Buf()
        P.dma("sp", mask_f[:], c_mask[:, :], writes=[b1])
        bm = Buf()
        cast("dve", mask[:], mask_f[:], [b1], [bm])
        bo = Buf()
        P.op("pool", lambda g: g.memset(ones[:], 1.0), writes=[bo])
        pos = sbt(es, [128, NB])
        iot = sbt(es, [128, 32])
        bp = Buf()
        bio = Buf()
        P.dma("sp", pos[:], c_pos[:, :], writes=[bp])
        P.dma("sp", iot[:], c_iota[:, :], writes=[bio])
        inv = sbt(es, [128, 32])
        binv = Buf()
        P.op("act", lambda g: g.activation(inv[:], iot[:], AF.Exp, scale=-math.log(10000.0) / 32.0),
             reads=[bio], writes=[binv])
        ytab = sbt(es, [128, NB * 64])
        by = Buf()
        P.op("dve", lambda g: g.tensor_scalar(inv[:], inv[:], float(1.0 / (2 * math.pi)), None, ALU.mult),
             reads=[binv], writes=[binv])
        for j in range(NB):
            P.op("dve", lambda g: g.tensor_scalar(ytab[:, j * 64 + 32: j * 64 + 64], inv[:], pos[:, j:j + 1],
                                                   None, ALU.mult),
                 reads=[binv, bp], writes=[by])
            P.op("dve", lambda g: g.tensor_scalar(ytab[:, j * 64: j * 64 + 32], ytab[:, j * 64 + 32: j * 64 + 64],
                                                   0.25, None, ALU.add),
                 reads=[by], writes=[by])
        yi = sbt(es, [128, NB * 64], I32)
        byi = Buf()
        P.op("dve", lambda g: g.tensor_copy(yi[:], ytab[:]), reads=[by], writes=[byi])
        yf = sbt(es, [128, NB * 64])
        byf = Buf()
        P.op("dve", lambda g: g.tensor_copy(yf[:], yi[:]), reads=[byi], writes=[byf])
        P.op("dve", lambda g: g.tensor_tensor(ytab[:], ytab[:], yf[:], ALU.subtract), reads=[by, byf], writes=[by])
        P.op("act", lambda g: g.activation(yf[:], ytab[:], AF.Sin, scale=6.283185), reads=[by], writes=[byf])
        brope = Buf()
        P.dma("sp", rope_d.rearrange("p j c -> p (j c)"), yf[:], reads=[byf], writes=[brope])
        memKT = sbt(es, [128, 8 * 256], BF16)
        memV = sbt(es, [128, 2 * 1024], BF16)
        wkv = sbt(es, [128, 8 * 2048], BF16)
        wb = []
        load_w_rows(wkv[:], W["mem_w_kv"], 8, 2048, wb)
        memf = sbt(es, [128, 2 * 1024])
        bmf = Buf()
        P.dma("sp", memf[:].rearrange("p (s f) -> p s f", s=2),
              mem_in.rearrange("(s p) f -> p s f", p=128), writes=[bmf])
        memb = sbt(es, [128, 2 * 1024], BF16)
        bmb = Buf()
        cast("pool", memb[:], memf[:], [bmf], [bmb])
        memT = sbt(es, [128, 8 * 256], BF16)
        bmT = Buf()
        for s in range(2):
            for c in range(8):
                P.op("pe", lambda g: g.transpose(psh(0)[:, c * 128:(c + 1) * 128],
                                                 memb[:, s * 1024 + c * 128: s * 1024 + (c + 1) * 128], ident[:]),
                     reads=[bmb, bi], writes=[BPS[0]], sig=(c == 7))
            P.op("act", lambda g: g.copy(
                memT[:].rearrange("p (c t) -> p c t", c=8)[:, :, s * 128:(s + 1) * 128],
                psh(0).rearrange("p (c t) -> p c t", c=8)), reads=[BPS[0]], writes=[bmT])
        P.join(["pe"], wb)
        bk = Buf()
        for m in range(8):
            pb = 1 + (m % 2)
            for k in range(8):
                P.op("pe", lambda g: g.matmul(psf(pb)[:, 0:256], wkv[:, k * 2048 + m * 128: k * 2048 + (m + 1) * 128],
                                              memT[:, k * 256:(k + 1) * 256], start=(k == 0), stop=(k == 7)),
                     reads=[bmT], writes=[BPS[pb]], sig=(k == 7))
            P.op("act", lambda g: g.copy(memKT[:, m * 256:(m + 1) * 256], psf(pb)[:, 0:256]),
                 reads=[BPS[pb]], writes=[bk])
        for mc in range(2):
            for n in range(2):
                pb = 3 + n
                for k in range(8):
                    P.op("pe", lambda g: g.matmul(psf(pb), memT[:, k * 256 + mc * 128: k * 256 + (mc + 1) * 128],
                                                  wkv[:, k * 2048 + 1024 + n * 512: k * 2048 + 1024 + (n + 1) * 512],
                                                  start=(k == 0), stop=(k == 7)),
                         reads=[bmT], writes=[BPS[pb]], sig=(k == 7))
                P.op("act", lambda g: g.copy(memV[:, mc * 1024 + n * 512: mc * 1024 + (n + 1) * 512], psf(pb)),
                     reads=[BPS[pb]], writes=[bk])
        bmem = Buf()
        P.dma("sp", memKT_d[:, :], memKT[:], reads=[bk], writes=[bmem])
        P.dma("sp", memV_d[:, :], memV[:], reads=[bk], writes=[bmem])
        P.barrier()

    def load_ln(es, li, j):
        gam = sbt(es, [128, 1024])
        bet = sbt(es, [128, 1024])
        epsc = sbt(es, [128, 1])
        b = Buf()
        P.dma("sp", gam[:], W["ln_g"][li, j].partition_broadcast(128), writes=[b])
        P.dma("sp", bet[:], W["ln_b"][li, j].partition_broadcast(128), writes=[b])
        P.op("pool", lambda g: g.memset(epsc[:], LN_EPS), writes=[b])
        return gam, bet, epsc, b

    class LNState:
        pass

    def make_ln(es, li, j, nz=2):
        st = LNState()
        st.gam, st.bet, st.eps, st.bpar = load_ln(es, li, j)
        st.z = [sbt(es, [128, 1024]) for _ in range(nz)]
        st.bz = [Buf() for _ in range(nz)]
        st.stats = [sbt(es, [128, 16]) for _ in range(2)]
        st.bst = [Buf() for _ in range(2)]
        st.n = 0
        st.pending = []
        return st

    def ln_flush(st, keep=0):
        while len(st.pending) > keep:
            st.pending.pop(0)()

    def ln_epilogue(st, py, xres, bxres, dst, bdst):
        i = st.n % 2
        st.n += 1
        z, bz, sx, bs = st.z[i % len(st.z)], st.bz[i % len(st.z)], st.stats[i], st.bst[i]
        for n in range(2):
            P.op("dve", lambda g: g.scalar_tensor_tensor(z[:, n * 512:(n + 1) * 512], xres[:, n * 512:(n + 1) * 512],
                                                         float(ALPHA), psf(py[n]), ALU.mult, ALU.add),
                 reads=[bxres, BPS[py[n]]], writes=[bz])
        for n in range(2):
            P.op("dve", lambda g: g.bn_stats(sx[:, n * 6:(n + 1) * 6], z[:, n * 512:(n + 1) * 512]),
                 reads=[bz], writes=[bs])
        P.op("dve", lambda g: g.bn_aggr(sx[:, 12:14], sx[:, 0:12]), reads=[bs], writes=[bs])
        P.op("act", lambda g: g.activation(sx[:, 14:15], sx[:, 13:14], AF.Ln, bias=st.eps[:, 0:1], scale=1.0),
             reads=[bs, st.bpar], writes=[bs])
        P.op("act", lambda g: g.activation(sx[:, 14:15], sx[:, 14:15], AF.Exp, scale=-0.5), reads=[bs], writes=[bs])
        def finish():
            P.op("dve", lambda g: g.scalar_tensor_tensor(sx[:, 15:16], sx[:, 12:13], -1.0, sx[:, 14:15], ALU.mult, ALU.mult),
                 reads=[bs], writes=[bs])
            P.op("act", lambda g: g.activation(z[:], z[:], AF.Identity, bias=sx[:, 15:16], scale=sx[:, 14:15]),
                 reads=[bz, bs], writes=[bz])
            P.op("pool", lambda g: g.tensor_tensor(z[:], z[:], st.gam[:], ALU.mult), reads=[bz, st.bpar], writes=[bz])
            P.op("pool", lambda g: g.tensor_tensor(z[:], z[:], st.bet[:], ALU.add), reads=[bz, st.bpar], writes=[bz])
            P.dma("sp", dst, z[:], reads=[bz], writes=[bdst])

        st.pending.append(finish)
        ln_flush(st, len(st.z) - 1)

    class XT:
        pass

    def make_xt(es, nsub, nbuf=2, nxt=2, nxb=2):
        o = XT()
        o.nsub = nsub
        o.xin = [sbt(es, [128, nsub * 1024]) for _ in range(nbuf)]
        o.bxin = [Buf() for _ in range(nbuf)]
        o.xb = [sbt(es, [128, 1024], BF16) for _ in range(nxb)]
        o.bxb = [Buf() for _ in range(nxb)]
        o.ncast = 0
        o.xT = [sbt(es, [128, 8 * nsub * 128], BF16) for _ in range(nxt)]
        o.bxT = [Buf() for _ in range(nxt)]
        o.n = 0
        o.nissue = 0
        o.issued = {}
        return o

    def issue_xt(o, src, src_bufs, sub0):
        if sub0 in o.issued or sub0 * 128 >= T:
            return
        i = o.nissue % len(o.xin)
        o.nissue += 1
        o.issued[sub0] = i
        ns = o.nsub
        P.dma("sp", o.xin[i][:].rearrange("p (s f) -> p s f", s=ns),
              src[sub0 * 128:(sub0 + ns) * 128, :].rearrange("(s p) f -> p s f", p=128),
              reads=[src_bufs[sub0 + s] for s in range(ns)] if src_bufs else [], writes=[o.bxin[i]])

    def prep_xt(o, src, src_bufs, sub0, trbank, prefetch=True):
        issue_xt(o, src, src_bufs, sub0)
        i = o.issued[sub0]
        o.n += 1
        ns = o.nsub
        xin, bxin = o.xin[i], o.bxin[i]
        xT, bxT = o.xT[(o.n - 1) % len(o.xT)], o.bxT[(o.n - 1) % len(o.xT)]
        if len(o.xin) > 1 and prefetch:
            issue_xt(o, src, src_bufs, sub0 + ns)
        W_ = ns * 128
        for s in range(ns):
            xb_, bxb_ = o.xb[o.ncast % len(o.xb)], o.bxb[o.ncast % len(o.xb)]
            o.ncast += 1
            cast("dve", xb_[:], xin[:, s * 1024:(s + 1) * 1024], [bxin], [bxb_])
            for c in range(8):
                P.op("pe", lambda g: g.transpose(psh(trbank)[:, c * 128:(c + 1) * 128],
                                                 xb_[:, c * 128:(c + 1) * 128], ident[:]),
                     reads=[bxb_], writes=[BPS[trbank]], sig=(c == 7))
            P.op("act", lambda g: g.copy(
                xT[:].rearrange("p (c t) -> p c t", c=8)[:, :, s * 128:(s + 1) * 128],
                psh(trbank).rearrange("p (c t) -> p c t", c=8)), reads=[BPS[trbank]], writes=[bxT])
        return xin, bxin, xT, bxT, W_

    def dst_for(sub_idx, last):
        if last:
            return out_d[sub_idx * 128:(sub_idx + 1) * 128, :], out_b[sub_idx]
        return xs_d[sub_idx * 128:(sub_idx + 1) * 128, :], xs_b[sub_idx]

    def phase_xattn(li, src, src_bufs, last):
        with ExitStack() as es:
            wq = sbt(es, [128, 8 * 1024], BF16)
            wo = sbt(es, [128, 8 * 1024], BF16)
            wb = []
            load_w_rows(wq[:], W["x_w_q"][li], 8, 1024, wb)
            load_w_rows(wo[:], W["x_w_o"][li], 8, 1024, wb)
            memKT = sbt(es, [128, 8 * 256], BF16)
            memV = sbt(es, [128, 2 * 1024], BF16)
            bmkv = Buf()
            P.dma("sp", memKT[:], memKT_d[:, :], reads=[bmem], writes=[bmkv])
            P.dma("sp", memV[:], memV_d[:, :], reads=[bmem], writes=[bmkv])
            wb.append(bmkv)
            ln = make_ln(es, li, 1)
            xt = make_xt(es, 4)
            qT = sbt(es, [128, 8 * 512], BF16)
            bq = Buf()
            pT = [sbt(es, [128, 512], BF16) for _ in range(4)]
            bpT = [Buf() for _ in range(4)]
            rden = [sbt(es, [128, 512]) for _ in range(2)]
            brd = [Buf(), Buf()]
            oT = sbt(es, [128, 8 * 512], BF16)
            boT = Buf()
            P.join(["pe"], wb)
            npt = 0
            nxt_prep = prep_xt(xt, src, src_bufs, 0, 0, prefetch=False)
            issue_xt(xt, src, src_bufs, 4)
            for t in range(NB // 4):
                xin, bxin, xT, bxT, Wd = nxt_prep
                for m in range(8):
                    pb = m % 2
                    for k in range(8):
                        P.op("pe", lambda g: g.matmul(psf(pb), wq[:, k * 1024 + m * 128: k * 1024 + (m + 1) * 128],
                                                      xT[:, k * 512:(k + 1) * 512], start=(k == 0), stop=(k == 7)),
                             reads=[bxT], writes=[BPS[pb]], sig=(k == 7))
                    cast("dve" if m % 2 else "act", qT[:, m * 512:(m + 1) * 512], psf(pb), [BPS[pb]], [bq])
                if t + 1 < NB // 4:
                    nxt_prep = prep_xt(xt, src, src_bufs, (t + 1) * 4, 0, prefetch=False)
                def emit_S(h):
                    nonlocal npt
                    pts = []
                    for mc in range(2):
                        pb = (2 + mc) if h % 2 == 0 else mc
                        for dc in range(2):
                            m = 2 * h + dc
                            P.op("pe", lambda g: g.matmul(psf(pb), memKT[:, m * 256 + mc * 128: m * 256 + (mc + 1) * 128],
                                                          qT[:, m * 512:(m + 1) * 512], start=(dc == 0), stop=(dc == 1)),
                                 reads=[bq], writes=[BPS[pb]], sig=(dc == 1))
                        pi = npt % 4
                        npt += 1
                        P.op("act", lambda g: g.activation(pT[pi][:], psf(pb), AF.Exp, scale=1.0 / 16.0),
                             reads=[BPS[pb]], writes=[bpT[pi]])
                        pts.append(pi)
                    return pts

                def emit_PV(h, pts):
                    for mc in range(2):
                        P.op("pe", lambda g: g.matmul(psf(4), ones[:], pT[pts[mc]][:], start=(mc == 0), stop=(mc == 1)),
                             reads=[bpT[pts[mc]]], writes=[BPS[4]], sig=(mc == 1))
                    P.op("act", lambda g: g.activation(rden[h % 2][:], psf(4), AF.Ln), reads=[BPS[4]], writes=[brd[h % 2]])
                    P.op("act", lambda g: g.activation(rden[h % 2][:], rden[h % 2][:], AF.Exp, scale=-1.0),
                         reads=[brd[h % 2]], writes=[brd[h % 2]])
                    for vc in range(2):
                        pb = 5 + vc
                        for mc in range(2):
                            P.op("pe", lambda g: g.matmul(
                                psf(pb), memV[:, mc * 1024 + h * 256 + vc * 128: mc * 1024 + h * 256 + (vc + 1) * 128],
                                pT[pts[mc]][:], start=(mc == 0), stop=(mc == 1)),
                                reads=[bpT[pts[mc]]], writes=[BPS[pb]], sig=(mc == 1))
                        m = 2 * h + vc
                        P.op("dve", lambda g: g.tensor_tensor(oT[:, m * 512:(m + 1) * 512], psf(pb), rden[h % 2][:], ALU.mult),
                             reads=[BPS[pb], brd[h % 2]], writes=[boT])

                pend = emit_S(0)
                for h in range(4):
                    nxt = emit_S(h + 1) if h + 1 < 4 else None
                    emit_PV(h, pend)
                    pend = nxt
                for s in range(4):
                    ybanks = (6, 7) if s % 2 == 0 else (4, 5)
                    for n in range(2):
                        pb = ybanks[n]
                        for k in range(8):
                            P.op("pe", lambda g: g.matmul(psf(pb), oT[:, k * 512 + s * 128: k * 512 + (s + 1) * 128],
                                                          wo[:, k * 1024 + n * 512: k * 1024 + (n + 1) * 512],
                                                          start=(k == 0), stop=(k == 7)),
                                 reads=[boT], writes=[BPS[pb]], sig=(k == 7))
                    dst, bdst = dst_for(t * 4 + s, last)
                    ln_epilogue(ln, ybanks, xin[:, s * 1024:(s + 1) * 1024], bxin, dst, bdst)
                issue_xt(xt, src, src_bufs, (t + 2) * 4)
            ln_flush(ln)
            P.barrier()

    def phase_ffn(li, src, src_bufs, last):
        with ExitStack() as es:
            wup = sbt(es, [128, 8 * 5632], BF16)
            wdn = sbt(es, [128, 22 * 1024], BF16)
            wb = []
            load_w_rows(wup[:], W["f_w_up"][li], 8, 5632, wb)
            load_w_rows(wdn[:], W["f_w_down"][li], 22, 1024, wb)
            cw = sbt(es, [128, 44 * 4])
            bcw = Buf()
            cw3 = cw[:].rearrange("p (m k) -> p m k", k=4)
            for k in range(3):
                P.dma("sp", cw3[:, :, k], W["f_conv_w"][li, k].rearrange("(m p) -> p m", p=128), writes=[bcw],
                      allow_slow_non_contiguous=True)
            P.dma("sp", cw3[:, :, 3], W["f_conv_b"][li].rearrange("(m p) -> p m", p=128), writes=[bcw],
                  allow_slow_non_contiguous=True)
            ln = make_ln(es, li, 2, nz=2)
            NS = 2
            TW = NS * 128
            xt = make_xt(es, NS, nbuf=2, nxt=1, nxb=1)
            hT = sbt(es, [128, 22 * TW], BF16)
            bhT = Buf()
            halo = sbt(es, [128, 44 * 2])
            bhalo = [Buf() for _ in range(44)]
            P.op("pool", lambda g: g.memset(halo[:], 0.0), writes=bhalo)
            stg = [sbt(es, [128, TW + 2]) for _ in range(4)]
            bstg = [Buf() for _ in range(4)]
            bstgh = [Buf() for _ in range(4)]
            NCV = 6
            cv = [sbt(es, [128, TW]) for _ in range(NCV)]
            bcv = [Buf() for _ in range(NCV)]
            P.join(ALLC, wb + [bcw])
            nxt_prep = prep_xt(xt, src, src_bufs, 0, 0, prefetch=False)
            issue_xt(xt, src, src_bufs, NS)
            for t in range(NB // NS):
                xin, bxin, xT, bxT, Wd = nxt_prep

                def S0(m):
                    for half in range(2):
                        mm = m + 22 * half
                        pb = 1 + ((2 * m + half) % 4)
                        for k in range(8):
                            P.op("pe", lambda g: g.matmul(psf(pb)[:, 0:TW],
                                                          wup[:, k * 5632 + mm * 128: k * 5632 + (mm + 1) * 128],
                                                          xT[:, k * TW:(k + 1) * TW], start=(k == 0), stop=(k == 7)),
                                 reads=[bxT], writes=[BPS[pb]], sig=(k == 7))

                def S1(m):
                    hs = []
                    for half in range(2):
                        mm = m + 22 * half
                        pb = 1 + ((2 * m + half) % 4)
                        ix = (2 * m + half) % 4
                        hs.append((mm, pb, stg[ix], bstg[ix], bstgh[ix], cv[(2 * m + half) % NCV], bcv[(2 * m + half) % NCV]))
                    for (mm, pb, sg, bsg, bsgh, c_, bc_) in hs:
                        P.op("pool", lambda g: g.tensor_copy(sg[:, 0:2], halo[:, mm * 2:mm * 2 + 2]),
                             reads=[bhalo[mm]], writes=[bsgh])
                    for (mm, pb, sg, bsg, bsgh, c_, bc_) in hs:
                        P.op("act", lambda g: g.copy(sg[:, 2:TW + 2], psf(pb)[:, 0:TW]), reads=[BPS[pb]], writes=[bsg])
                    for (mm, pb, sg, bsg, bsgh, c_, bc_) in hs:
                        P.op("pool", lambda g: g.tensor_copy(halo[:, mm * 2:mm * 2 + 2], sg[:, TW:TW + 2]),
                             reads=[bsg], writes=[bhalo[mm]])
                    for (mm, pb, sg, bsg, bsgh, c_, bc_) in hs:
                        P.op("act", lambda g: g.activation(c_[:], sg[:, 2:TW + 2], AF.Identity,
                                                           bias=cw[:, mm * 4 + 3: mm * 4 + 4],
                                                           scale=cw[:, mm * 4 + 2: mm * 4 + 3]),
                             reads=[bsg], writes=[bc_])

                def S2(m):
                    hs = []
                    for half in range(2):
                        mm = m + 22 * half
                        ix = (2 * m + half) % 4
                        hs.append((mm, stg[ix], bstg[ix], bstgh[ix], cv[(2 * m + half) % NCV], bcv[(2 * m + half) % NCV]))
                    for (mm, sg, bsg, bsgh, c_, bc_) in hs:
                        P.op("dve", lambda g: g.scalar_tensor_tensor(c_[:], sg[:, 1:TW + 1], cw[:, mm * 4 + 1: mm * 4 + 2],
                                                                     c_[:], ALU.mult, ALU.add),
                             reads=[bsg, bsgh, bc_], writes=[bc_])
                    for (mm, sg, bsg, bsgh, c_, bc_) in hs:
                        P.op("dve", lambda g: g.scalar_tensor_tensor(c_[:], sg[:, 0:TW], cw[:, mm * 4: mm * 4 + 1],
                                                                     c_[:], ALU.mult, ALU.add),
                             reads=[bsg, bsgh, bc_], writes=[bc_])

                def S3(m):
                    cg, bcg = cv[(2 * m) % NCV], bcv[(2 * m) % NCV]
                    cu, bcu = cv[(2 * m + 1) % NCV], bcv[(2 * m + 1) % NCV]
                    P.op("act", lambda g: g.activation(cg[:], cg[:], AF.Silu), reads=[bcg], writes=[bcg])
                    P.op("pool", lambda g: g.tensor_tensor(hT[:, m * TW:(m + 1) * TW], cg[:], cu[:], ALU.mult),
                         reads=[bcg, bcu], writes=[bhT])

                for i in range(22 + 3):
                    if i < 22:
                        S0(i)
                    if i == 22 and t + 1 < NB // NS:
                        nxt_prep = prep_xt(xt, src, src_bufs, (t + 1) * NS, 0, prefetch=False)
                    if 0 <= i - 1 < 22:
                        S1(i - 1)
                    if 0 <= i - 2 < 22:
                        S2(i - 2)
                    if 0 <= i - 3 < 22:
                        S3(i - 3)
                for s in range(NS):
                    ybanks = (5, 6) if s % 2 == 0 else (7, 0)
                    for n in range(2):
                        pb = ybanks[n]
                        for k in range(22):
                            P.op("pe", lambda g: g.matmul(psf(pb), hT[:, k * TW + s * 128: k * TW + (s + 1) * 128],
                                                          wdn[:, k * 1024 + n * 512: k * 1024 + (n + 1) * 512],
                                                          start=(k == 0), stop=(k == 21)),
                                 reads=[bhT], writes=[BPS[pb]], sig=(k == 21))
                    dst, bdst = dst_for(t * NS + s, last)
                    ln_epilogue(ln, ybanks, xin[:, s * 1024:(s + 1) * 1024], bxin, dst, bdst)
                issue_xt(xt, src, src_bufs, (t + 2) * NS)
            ln_flush(ln)
            P.barrier()

    def phase_swa(li, src, src_bufs, last):
        j = li // 3
        with ExitStack() as es:
            wqkv = sbt(es, [128, 8 * 1536], BF16)
            wo = sbt(es, [128, 8 * 1024], BF16)
            wb = []
            load_w_rows(wqkv[:], W["a_w_qkv"][j], 8, 1536, wb)
            load_w_rows(wo[:], W["a_w_o"][j], 8, 1024, wb)
            PERM = (0, 2, 1, 3)
            sk = sbt(es, [128, 16])
            bsk = Buf()
            P.dma("sp", sk[:], W["a_sinks"][j].partition_broadcast(128), writes=[bsk])
            P.op("act", lambda g: g.activation(sk[:], sk[:], AF.Exp), reads=[bsk], writes=[bsk])
            esk = sbt(es, [128, 16 * 128])
            esk4 = esk[:].rearrange("p (g r t) -> p g r t", g=4, r=4)
            sk4 = sk[:].rearrange("p (g r) -> p g r", g=4)
            for r_ in range(4):
                P.op("dve", lambda g: g.tensor_copy(esk4[:, :, PERM[r_], :],
                                                    sk4[:, :, r_].unsqueeze(2).broadcast_to([128, 4, 128])),
                     reads=[bsk], writes=[bsk])
            rope = sbt(es, [128, NB * 64])
            brope_l = Buf()
            P.dma("sp", rope[:], rope_d.rearrange("p j c -> p (j c)"), reads=[brope], writes=[brope_l])
            ln = make_ln(es, li, 0)
            xt = make_xt(es, 1, nbuf=4, nxt=2)
            qks = [sbt(es, [128, 1280]) for _ in range(2)]
            bqks = [Buf(), Buf()]
            t1 = sbt(es, [128, 640])
            t2 = sbt(es, [128, 640])
            bt1 = Buf()
            bt2 = Buf()
            qkr = sbt(es, [128, 1280], BF16)
            bqkr = Buf()
            QT = [sbt(es, [64, 16 * 128], BF16) for _ in range(2)]
            bQT = [Buf(), Buf()]
            KT = [sbt(es, [64, 4 * 128], BF16) for _ in range(3)]
            bKT = [Buf() for _ in range(3)]
            V = [sbt(es, [128, 256], BF16) for _ in range(4)]
            bV = [Buf() for _ in range(4)]
            pT = [sbt(es, [128, 512], BF16) for _ in range(4)]
            bpT = [Buf() for _ in range(4)]
            den = sbt(es, [64, 512])
            bden = Buf()
            OT = sbt(es, [128, 8 * 128], BF16)
            bOT = Buf()
            ones64 = ones[:, 0:64]
            P.join(ALLC, wb + [bsk, brope_l])
            npt = 0
            scale = 64.0 ** -0.5

            def front1(b):
                qk, bqk = qks[b % 2], bqks[b % 2]
                xin, bxin, xT, bxT, Wd = prep_xt(xt, src, src_bufs, b, 7)
                for n in range(3):
                    for k in range(8):
                        P.op("pe", lambda g: g.matmul(psf(n), xT[:, k * 128:(k + 1) * 128],
                                                      wqkv[:, k * 1536 + n * 512: k * 1536 + (n + 1) * 512],
                                                      start=(k == 0), stop=(k == 7)),
                             reads=[bxT], writes=[BPS[n]], sig=(k == 7))
                P.op("act", lambda g: g.copy(qk[:, 0:512], psf(0)), reads=[BPS[0]], writes=[bqk])
                P.op("act", lambda g: g.copy(qk[:, 512:1024], psf(1)), reads=[BPS[1]], writes=[bqk])
                P.op("act", lambda g: g.copy(qk[:, 1024:1280], psf(2)[:, 0:256]), reads=[BPS[2]], writes=[bqk])
                P.op("act", lambda g: g.copy(V[b % 4][:], psf(2)[:, 256:512]), reads=[BPS[2]], writes=[bV[b % 4]])
                return xin, bxin

            def front2(b):
                cur = b % 3
                qk, bqk = qks[b % 2], bqks[b % 2]
                q4 = qk[:].rearrange("p (h two d) -> p h two d", h=20, two=2)
                o4 = qkr[:].rearrange("p (h two d) -> p h two d", h=20, two=2)
                x1, x2 = q4[:, :, 0, :], q4[:, :, 1, :]
                cosb = rope[:, b * 64: b * 64 + 32].unsqueeze(1).broadcast_to([128, 20, 32])
                sinb = rope[:, b * 64 + 32: b * 64 + 64].unsqueeze(1).broadcast_to([128, 20, 32])
                t1v = t1[:].rearrange("p (h d) -> p h d", h=20)
                t2v = t2[:].rearrange("p (h d) -> p h d", h=20)
                P.op("dve", lambda g: g.tensor_tensor(t1v, x1, cosb, ALU.mult), reads=[bqk], writes=[bt1])
                P.op("pool", lambda g: g.tensor_tensor(t2v, x2, sinb, ALU.mult), reads=[bqk], writes=[bt2])
                P.op("dve", lambda g: g.tensor_tensor(o4[:, :, 0, :], t1v, t2v, ALU.subtract),
                     reads=[bt1, bt2], writes=[bqkr])
                P.op("pool", lambda g: g.tensor_tensor(t1v, x2, cosb, ALU.mult), reads=[bqk], writes=[bt1])
                P.op("dve", lambda g: g.tensor_tensor(t2v, x1, sinb, ALU.mult), reads=[bqk], writes=[bt2])
                P.op("pool", lambda g: g.tensor_tensor(o4[:, :, 1, :], t1v, t2v, ALU.add),
                     reads=[bt1, bt2], writes=[bqkr])
                for h in range(20):
                    pb = h // 8
                    sl = (h % 8) if h >= 16 else 4 * ((h % 8) // 4) + PERM[h % 4]
                    P.op("pe", lambda g: g.transpose(psh(pb)[0:64, sl * 128:(sl + 1) * 128],
                                                     qkr[:, h * 64:(h + 1) * 64], ident[:]),
                         reads=[bqkr], writes=[BPS[pb]], sig=(h % 8 == 7 or h == 19))
                QTb, bQTb = QT[b % 2], bQT[b % 2]
                P.op("act", lambda g: g.copy(QTb[:, 0:1024], psh(0)[0:64, :]), reads=[BPS[0]], writes=[bQTb])
                P.op("dve", lambda g: g.tensor_copy(QTb[:, 1024:2048], psh(1)[0:64, :]), reads=[BPS[1]], writes=[bQTb])
                P.op("act", lambda g: g.copy(KT[cur][:], psh(2)[0:64, 0:512]), reads=[BPS[2]], writes=[bKT[cur]])

            def back(b, xin, bxin):
                nonlocal npt
                cur = b % 3
                prv = (b - 1) % 3
                QTb, bQTb = QT[b % 2], bQT[b % 2]
                chunks = [(cur, b % 4, mask[:, 0:128])]
                if b > 0:
                    chunks.append((prv, (b - 1) % 4, mask[:, 128:256]))
                nchk = len(chunks)

                def emit_S(gk):
                    nonlocal npt
                    pis = []
                    for (slot, vslot, mk) in chunks:
                        pb = 3 + (npt % 2)
                        pi = npt % 4
                        npt += 1
                        P.op("pe", lambda g: g.matmul(psf(pb), KT[slot][:, gk * 128:(gk + 1) * 128],
                                                      QTb[:, gk * 512:(gk + 1) * 512], start=True, stop=True),
                             reads=[bKT[slot], bQTb], writes=[BPS[pb]], sig=True)
                        P.op("act", lambda g: g.activation(pT[pi][:], psf(pb), AF.Exp, scale=scale),
                             reads=[BPS[pb]], writes=[bpT[pi]])
                        P.op("pool", lambda g: g.tensor_tensor(
                            pT[pi][:].rearrange("p (h t) -> p h t", h=4), pT[pi][:].rearrange("p (h t) -> p h t", h=4),
                            mk.unsqueeze(1).broadcast_to([128, 4, 128]), ALU.mult),
                            reads=[bpT[pi], bm], writes=[bpT[pi]])
                        pis.append((pi, vslot))
                    return pis

                def emit_OD(gk, pis):
                    for ci, (pi, slot) in enumerate(pis):
                        P.op("pe", lambda g: g.matmul(psf(5)[0:64, :], V[slot][:, gk * 64:(gk + 1) * 64], pT[pi][:],
                                                      start=(ci == 0), stop=(ci == nchk - 1)),
                             reads=[bV[slot], bpT[pi]], writes=[BPS[5]], sig=(ci == nchk - 1))
                    for ci, (pi, slot) in enumerate(pis):
                        P.op("pe", lambda g: g.matmul(psf(6)[0:64, :], ones64, pT[pi][:],
                                                      start=(ci == 0), stop=(ci == nchk - 1)),
                             reads=[bpT[pi]], writes=[BPS[6]], sig=(ci == nchk - 1))
                    P.op("dve", lambda g: g.tensor_tensor(den[:], psf(6)[0:64, :], esk[0:64, gk * 512:(gk + 1) * 512],
                                                          ALU.add), reads=[BPS[6]], writes=[bden])
                    P.op("act", lambda g: g.activation(den[:], den[:], AF.Ln), reads=[bden], writes=[bden])
                    P.op("act", lambda g: g.activation(den[:], den[:], AF.Exp, scale=-1.0), reads=[bden], writes=[bden])
                    P.op("dve", lambda g: g.tensor_tensor(OT[0:64, gk * 256:(gk + 1) * 256], psf(5)[0:64, 0:256], den[:, 0:256],
                                                          ALU.mult), reads=[BPS[5], bden], writes=[bOT])
                    P.op("dve", lambda g: g.tensor_tensor(OT[64:128, gk * 256:(gk + 1) * 256], psf(5)[0:64, 256:512],
                                                          den[:, 256:512], ALU.mult), reads=[BPS[5], bden], writes=[bOT])

                pend = emit_S(0)
                for gk in range(4):
                    nxt = emit_S(gk + 1) if gk + 1 < 4 else None
                    emit_OD(gk, pend)
                    pend = nxt
                for n in range(2):
                    pb = 3 + n
                    for jp in range(8):
                        P.op("pe", lambda g: g.matmul(psf(pb), OT[:, jp * 128:(jp + 1) * 128],
                                                      wo[:, jp * 1024 + n * 512: jp * 1024 + (n + 1) * 512],
                                                      start=(jp == 0), stop=(jp == 7)),
                             reads=[bOT], writes=[BPS[pb]], sig=(jp == 7))
                dst, bdst = dst_for(b, last)
                ln_epilogue(ln, (3, 4), xin[:, 0:1024], bxin, dst, bdst)

            sts = {}
            sts[0] = front1(0)
            if NB > 1:
                sts[1] = front1(1)
            front2(0)
            for b in range(NB):
                if b + 2 < NB:
                    sts[b + 2] = front1(b + 2)
                if b + 1 < NB:
                    front2(b + 1)
                back(b, *sts.pop(b))
            ln_flush(ln)
            P.barrier()

    def phase_lru(li, src, src_bufs, last):
        with ExitStack() as es:
            win = sbt(es, [128, 8 * 2048], BF16)
            wr = sbt(es, [128, 8 * 256], BF16)
            wi = sbt(es, [128, 8 * 256], BF16)
            wo = sbt(es, [128, 8 * 1024], BF16)
            wb = []
            load_w_rows(win[:], W["b_w_in"][0], 8, 2048, wb)
            for blk in range(4):
                for ic in range(2):
                    c = blk * 2 + ic
                    load_cast(wr[:, c * 256:(c + 1) * 256], W["b_w_rgate"][0, blk, ic * 128:(ic + 1) * 128, :], wb)
                    load_cast(wi[:, c * 256:(c + 1) * 256], W["b_w_igate"][0, blk, ic * 128:(ic + 1) * 128, :], wb)
            load_w_rows(wo[:], W["b_w_o"][0], 8, 1024, wb)
            par = sbt(es, [128, 64])
            bpar = Buf()
            cwv = par[:, 0:32].rearrange("p (c k) -> p c k", k=4)
            for k in range(4):
                P.dma("sp", cwv[:, :, k], W["b_conv_w"][0, k].rearrange("(c p) -> p c", p=128), writes=[bpar],
                      allow_slow_non_contiguous=True)
            for nm, o_ in (("b_conv_b", 32), ("b_b_rgate", 40), ("b_b_igate", 48), ("b_lambda", 56)):
                P.dma("sp", par[:, o_:o_ + 8], W[nm][0].rearrange("(c p) -> p c", p=128), writes=[bpar],
                      allow_slow_non_contiguous=True)
            p2 = sbt(es, [128, 32])
            P.op("dve", lambda g: g.tensor_scalar(p2[:, 0:16], par[:, 40:56], 0.5, None, ALU.mult),
                 reads=[bpar], writes=[bpar])
            P.op("act", lambda g: g.activation(p2[:, 24:32], par[:, 56:64], AF.Exp, scale=-1.0), reads=[bpar], writes=[bpar])
            P.op("act", lambda g: g.activation(p2[:, 24:32], p2[:, 24:32], AF.Ln, bias=1.0, scale=1.0),
                 reads=[bpar], writes=[bpar])
            P.op("dve", lambda g: g.tensor_scalar(p2[:, 16:24], p2[:, 24:32], -4.0, None, ALU.mult),
                 reads=[bpar], writes=[bpar])
            ln = make_ln(es, li, 0)
            NS = 2
            TW = NS * 128
            xt = make_xt(es, NS)
            gT = sbt(es, [128, 8 * TW])
            bgT = Buf()
            ucT = sbt(es, [128, 8 * TW])
            bucT = Buf()
            ucb = sbt(es, [128, 8 * TW], BF16)
            bucb = Buf()
            aT = sbt(es, [128, 8 * TW])
            baT = Buf()
            iuT = sbt(es, [128, 8 * TW])
            biuT = Buf()
            sqT = sbt(es, [128, 8 * TW])
            bsqT = Buf()
            hT = sbt(es, [128, 8 * TW])
            bhT = Buf()
            yT = sbt(es, [128, 8 * TW], BF16)
            byT = Buf()
            thrs = [sbt(es, [128, TW]) for _ in range(2)]
            bthrs = [Buf(), Buf()]
            this_ = [sbt(es, [128, TW]) for _ in range(2)]
            bthis = [Buf(), Buf()]
            halo = sbt(es, [128, 8 * 3])
            bhalo = Buf()
            hprev = sbt(es, [128, 8])
            bhp = Buf()
            P.op("pool", lambda g: g.memset(halo[:], 0.0), writes=[bhalo])
            P.op("pool", lambda g: g.memset(hprev[:], 0.0), writes=[bhp])
            stg = [sbt(es, [128, TW + 3]) for _ in range(2)]
            bstg = [Buf() for _ in range(2)]
            P.join(ALLC, wb + [bpar])
            nst = 0
            nxt_prep = prep_xt(xt, src, src_bufs, 0, 0, prefetch=False)
            issue_xt(xt, src, src_bufs, NS)
            for t in range(NB // NS):
                xin, bxin, xT, bxT, Wd = nxt_prep
                for m in range(8):
                    pb = 1 + (m % 2)
                    for k in range(8):
                        P.op("pe", lambda g: g.matmul(psf(pb)[:, 0:TW], win[:, k * 2048 + m * 128: k * 2048 + (m + 1) * 128],
                                                      xT[:, k * TW:(k + 1) * TW], start=(k == 0), stop=(k == 7)),
                             reads=[bxT], writes=[BPS[pb]], sig=(k == 7))
                    P.op("act", lambda g: g.activation(gT[:, m * TW:(m + 1) * TW], psf(pb)[:, 0:TW], AF.Gelu_apprx_tanh),
                         reads=[BPS[pb]], writes=[bgT])
                for c in range(8):
                    m = 8 + c
                    pb = 1 + (c % 2)
                    si = nst % 2
                    nst += 1
                    for k in range(8):
                        P.op("pe", lambda g: g.matmul(psf(pb)[:, 0:TW], win[:, k * 2048 + m * 128: k * 2048 + (m + 1) * 128],
                                                      xT[:, k * TW:(k + 1) * TW], start=(k == 0), stop=(k == 7)),
                             reads=[bxT], writes=[BPS[pb]], sig=(k == 7))
                    sg, bsg = stg[si], bstg[si]
                    P.op("pool", lambda g: g.tensor_copy(sg[:, 0:3], halo[:, c * 3:c * 3 + 3]), reads=[bhalo], writes=[bsg])
                    P.op("act", lambda g: g.copy(sg[:, 3:TW + 3], psf(pb)[:, 0:TW]), reads=[BPS[pb]], writes=[bsg])
                    P.op("pool", lambda g: g.tensor_copy(halo[:, c * 3:c * 3 + 3], sg[:, TW:TW + 3]),
                         reads=[bsg], writes=[bhalo])
                    uc = ucT[:, c * TW:(c + 1) * TW]
                    P.op("act", lambda g: g.activation(uc, sg[:, 3:TW + 3], AF.Identity, bias=par[:, 32 + c:33 + c],
                                                       scale=par[:, c * 4 + 3:c * 4 + 4]), reads=[bsg], writes=[bucT])
                    for k in range(3):
                        P.op("dve", lambda g: g.scalar_tensor_tensor(uc, sg[:, k:TW + k], par[:, c * 4 + k:c * 4 + k + 1],
                                                                     uc, ALU.mult, ALU.add),
                             reads=[bsg, bucT], writes=[bucT])
                    P.op("pool", lambda g: g.tensor_copy(ucb[:, c * TW:(c + 1) * TW], uc), reads=[bucT], writes=[bucb])
                if t + 1 < NB // NS:
                    nxt_prep = prep_xt(xt, src, src_bufs, (t + 1) * NS, 0, prefetch=False)
                for blk in range(4):
                    for jc in range(2):
                        o_ = blk * 2 + jc
                        thr, bthr = thrs[o_ % 2], bthrs[o_ % 2]
                        thi, bthi = this_[o_ % 2], bthis[o_ % 2]
                        gb = (3, 4) if o_ % 2 == 0 else (5, 2)
                        for gi, (wg, pbg) in enumerate(((wr, gb[0]), (wi, gb[1]))):
                            for ic in range(2):
                                c = blk * 2 + ic
                                P.op("pe", lambda g: g.matmul(psf(pbg)[:, 0:TW], wg[:, c * 256 + jc * 128: c * 256 + (jc + 1) * 128],
                                                              ucb[:, c * TW:(c + 1) * TW], start=(ic == 0), stop=(ic == 1)),
                                     reads=[bucb], writes=[BPS[pbg]], sig=(ic == 1))
                        P.op("act", lambda g: g.activation(thr[:], psf(gb[0])[:, 0:TW], AF.Tanh, bias=p2[:, o_:o_ + 1], scale=0.5),
                             reads=[BPS[gb[0]]], writes=[bthr])
                        P.op("act", lambda g: g.activation(aT[:, o_ * TW:(o_ + 1) * TW], thr[:], AF.Exp,
                                                           bias=p2[:, 16 + o_:17 + o_], scale=p2[:, 16 + o_:17 + o_]),
                             reads=[bthr], writes=[baT])
                        P.op("act", lambda g: g.activation(thi[:], psf(gb[1])[:, 0:TW], AF.Tanh, bias=p2[:, 8 + o_:9 + o_], scale=0.5),
                             reads=[BPS[gb[1]]], writes=[bthi])
                        P.op("dve", lambda g: g.scalar_tensor_tensor(iuT[:, o_ * TW:(o_ + 1) * TW], thi[:], 1.0,
                                                                     ucT[:, o_ * TW:(o_ + 1) * TW], ALU.add, ALU.mult),
                             reads=[bthi, bucT], writes=[biuT])
                P.op("pool", lambda g: g.tensor_tensor(sqT[:], aT[:], aT[:], ALU.mult), reads=[baT], writes=[bsqT])
                P.op("act", lambda g: g.activation(sqT[:], sqT[:], AF.Ln, bias=1.0, scale=-1.0), reads=[bsqT], writes=[bsqT])
                P.op("act", lambda g: g.activation(sqT[:], sqT[:], AF.Exp, scale=0.5), reads=[bsqT], writes=[bsqT])
                P.op("dve", lambda g: g.scalar_tensor_tensor(sqT[:], sqT[:], 0.5, iuT[:], ALU.mult, ALU.mult),
                     reads=[bsqT, biuT], writes=[bsqT])
                for o_ in range(8):
                    P.op("dve", lambda g: g.tensor_tensor_scan(hT[:, o_ * TW:(o_ + 1) * TW], aT[:, o_ * TW:(o_ + 1) * TW],
                                                               sqT[:, o_ * TW:(o_ + 1) * TW], hprev[:, o_:o_ + 1],
                                                               ALU.mult, ALU.add),
                         reads=[baT, bsqT, bhp], writes=[bhT])
                P.op("dve", lambda g: g.tensor_copy(hprev[:], hT[:].rearrange("p (c t) -> p c t", c=8)[:, :, TW - 1]),
                     reads=[bhT], writes=[bhp])
                P.op("pool", lambda g: g.tensor_tensor(yT[:], hT[:], gT[:], ALU.mult), reads=[bhT, bgT], writes=[byT])
                for s in range(NS):
                    for n in range(2):
                        pb = 6 + n
                        for k in range(8):
                            P.op("pe", lambda g: g.matmul(psf(pb), yT[:, k * TW + s * 128: k * TW + (s + 1) * 128],
                                                          wo[:, k * 1024 + n * 512: k * 1024 + (n + 1) * 512],
                                                          start=(k == 0), stop=(k == 7)),
                                 reads=[byT], writes=[BPS[pb]], sig=(k == 7))
                    dst, bdst = dst_for(t * NS + s, last)
                    ln_epilogue(ln, (6, 7), xin[:, s * 1024:(s + 1) * 1024], bxin, dst, bdst)
                issue_xt(xt, src, src_bufs, (t + 2) * NS)
            ln_flush(ln)
            P.barrier()

    def phase_mla(li, src, src_bufs, last):
        NT = T // 512
        with ExitStack() as es:
            wdn = sbt(es, [128, 8 * 704], BF16)
            wuqn = sbt(es, [128, 3 * 1024], BF16)
            wuqr = sbt(es, [128, 3 * 512], BF16)
            wukk = sbt(es, [128, 2 * 1024], BF16)
            wukv = sbt(es, [128, 2 * 1024], BF16)
            wo = sbt(es, [128, 8 * 1024], BF16)
            wb = []
            load_w_rows(wdn[:], W["c_w_down"][0], 8, 704, wb)
            for kc in range(3):
                src3 = W["c_w_uq"][0, kc * 128:(kc + 1) * 128, :].rearrange("p (h d) -> p h d", h=8)
                load_cast(wuqn[:, kc * 1024:(kc + 1) * 1024].rearrange("p (h d) -> p h d", h=8), src3[:, :, 0:128], wb)
                load_cast(wuqr[:, kc * 512:(kc + 1) * 512].rearrange("p (h d) -> p h d", h=8), src3[:, :, 128:192], wb)
            for kc in range(2):
                src3 = W["c_w_ukv"][0, kc * 128:(kc + 1) * 128, :].rearrange("p (h d) -> p h d", h=8)
                load_cast(wukk[:, kc * 1024:(kc + 1) * 1024].rearrange("p (h d) -> p h d", h=8), src3[:, :, 0:128], wb)
                load_cast(wukv[:, kc * 1024:(kc + 1) * 1024].rearrange("p (h d) -> p h d", h=8), src3[:, :, 128:256], wb)
            load_w_rows(wo[:], W["c_w_o"][0], 8, 1024, wb)
            gn = sbt(es, [128, 640])
            bgn = Buf()
            P.dma("sp", gn[:, 0:384], W["c_q_norm"][0].partition_broadcast(128), writes=[bgn])
            P.dma("sp", gn[:, 384:640], W["c_kv_norm"][0].partition_broadcast(128), writes=[bgn])
            epsr = sbt(es, [128, 1])
            P.op("pool", lambda g: g.memset(epsr[:], RMS_EPS), writes=[bgn])
            rope = sbt(es, [128, NB * 64])
            brl = Buf()
            P.dma("sp", rope[:], rope_d.rearrange("p j c -> p (j c)"), reads=[brope], writes=[brl])
            ckvT = sbt(es, [128, 2 * T], BF16)
            bckvT = Buf()
            krT = sbt(es, [64, T], BF16)
            bkrT = Buf()
            P.join(ALLC, wb + [bgn, brl])
            scale = 192.0 ** -0.5
            with ExitStack() as esA:
                xt = make_xt(esA, 4)
                junk = sbt(esA, [128, 384])
                bjunk = Buf()
                ss = sbt(esA, [128, 4])
                bss = Buf()
                lat = sbt(esA, [128, 704], BF16)
                blat = Buf()
                tr = [sbt(esA, [128, 32]) for _ in range(2)]
                btr = [Buf(), Buf()]
                cqT = sbt(esA, [128, 3 * 512], BF16)
                bcqT = Buf()
                qrf = sbt(esA, [128, 512])
                bqrf = Buf()
                t1 = sbt(esA, [128, 256])
                t2 = sbt(esA, [128, 256])
                bt1 = Buf()
                bt2 = Buf()
                qrb = sbt(esA, [128, 512], BF16)
                bqrb = Buf()
                qrT = sbt(esA, [64, 8 * 512], BF16)
                bqrT = Buf()
                Vt = sbt(esA, [128, 4 * 1024], BF16)
                bVt = Buf()
                bscr = Buf()
                for t in range(NT):
                    xin, bxin, xT, bxT, Wd = prep_xt(xt, src, src_bufs, t * 4, 0)
                    for s in range(4):
                        blk = t * 4 + s
                        tok = slice(s * 128, (s + 1) * 128)
                        for (pb, c0, cn) in ((1, 0, 384), (2, 384, 320)):
                            for k in range(8):
                                P.op("pe", lambda g: g.matmul(psf(pb)[:, 0:cn], xT[:, k * 512 + s * 128: k * 512 + (s + 1) * 128],
                                                              wdn[:, k * 704 + c0: k * 704 + c0 + cn],
                                                              start=(k == 0), stop=(k == 7)),
                                     reads=[bxT], writes=[BPS[pb]], sig=(k == 7))
                        P.op("act", lambda g: g.activation(junk[:, 0:384], psf(1)[:, 0:384], AF.Square, accum_out=ss[:, 0:1]),
                             reads=[BPS[1]], writes=[bjunk, bss])
                        P.op("act", lambda g: g.activation(junk[:, 0:256], psf(2)[:, 0:256], AF.Square, accum_out=ss[:, 1:2]),
                             reads=[BPS[2]], writes=[bjunk, bss])
                        P.op("act", lambda g: g.activation(ss[:, 2:3], ss[:, 0:1], AF.Ln, bias=epsr[:, 0:1], scale=1.0 / 384.0),
                             reads=[bss], writes=[bss])
                        P.op("act", lambda g: g.activation(ss[:, 3:4], ss[:, 1:2], AF.Ln, bias=epsr[:, 0:1], scale=1.0 / 256.0),
                             reads=[bss], writes=[bss])
                        P.op("act", lambda g: g.activation(ss[:, 2:4], ss[:, 2:4], AF.Exp, scale=-0.5), reads=[bss], writes=[bss])
                        P.op("dve", lambda g: g.scalar_tensor_tensor(lat[:, 0:384], psf(1)[:, 0:384], ss[:, 2:3], gn[:, 0:384],
                                                                     ALU.mult, ALU.mult),
                             reads=[BPS[1], bss], writes=[blat])
                        P.op("dve", lambda g: g.scalar_tensor_tensor(lat[:, 384:640], psf(2)[:, 0:256], ss[:, 3:4], gn[:, 384:640],
                                                                     ALU.mult, ALU.mult),
                             reads=[BPS[2], bss], writes=[blat])
                        cs = rope[:, blk * 64: blk * 64 + 32]
                        sn = rope[:, blk * 64 + 32: blk * 64 + 64]
                        k1 = psf(2)[:, 256:288]
                        k2 = psf(2)[:, 288:320]
                        P.op("dve", lambda g: g.tensor_tensor(tr[0][:], k1, cs, ALU.mult), reads=[BPS[2]], writes=[btr[0]])
                        P.op("dve", lambda g: g.tensor_tensor(tr[1][:], k2, sn, ALU.mult), reads=[BPS[2]], writes=[btr[1]])
                        P.op("dve", lambda g: g.tensor_tensor(lat[:, 640:672], tr[0][:], tr[1][:], ALU.subtract),
                             reads=[btr[0], btr[1]], writes=[blat])
                        P.op("dve", lambda g: g.tensor_tensor(tr[0][:], k2, cs, ALU.mult), reads=[BPS[2]], writes=[btr[0]])
                        P.op("dve", lambda g: g.tensor_tensor(tr[1][:], k1, sn, ALU.mult), reads=[BPS[2]], writes=[btr[1]])
                        P.op("dve", lambda g: g.tensor_tensor(lat[:, 672:704], tr[0][:], tr[1][:], ALU.add),
                             reads=[btr[0], btr[1]], writes=[blat])
                        for c in range(5):
                            P.op("pe", lambda g: g.transpose(psh(3)[:, c * 128:(c + 1) * 128], lat[:, c * 128:(c + 1) * 128], ident[:]),
                                 reads=[blat], writes=[BPS[3]], sig=False)
                        P.op("pe", lambda g: g.transpose(psh(3)[0:64, 640:768], lat[:, 640:704], ident[:]),
                             reads=[blat], writes=[BPS[3]])
                        P.op("act", lambda g: g.copy(cqT[:].rearrange("p (c t) -> p c t", c=3)[:, :, tok],
                                                     psh(3)[:, 0:384].rearrange("p (c t) -> p c t", c=3)),
                             reads=[BPS[3]], writes=[bcqT])
                        P.op("act", lambda g: g.copy(
                            ckvT[:].rearrange("p (c t) -> p c t", c=2)[:, :, blk * 128:(blk + 1) * 128],
                            psh(3)[:, 384:640].rearrange("p (c t) -> p c t", c=2)), reads=[BPS[3]], writes=[bckvT])
                        P.op("act", lambda g: g.copy(krT[:, blk * 128:(blk + 1) * 128], psh(3)[0:64, 640:768]),
                             reads=[BPS[3]], writes=[bkrT])
                        for kc in range(3):
                            P.op("pe", lambda g: g.matmul(psf(4), cqT[:, kc * 512 + s * 128: kc * 512 + (s + 1) * 128],
                                                          wuqr[:, kc * 512:(kc + 1) * 512], start=(kc == 0), stop=(kc == 2)),
                                 reads=[bcqT], writes=[BPS[4]], sig=(kc == 2))
                        P.op("act", lambda g: g.copy(qrf[:], psf(4)), reads=[BPS[4]], writes=[bqrf])
                        q4 = qrf[:].rearrange("p (h two d) -> p h two d", h=8, two=2)
                        o4 = qrb[:].rearrange("p (h two d) -> p h two d", h=8, two=2)
                        x1, x2 = q4[:, :, 0, :], q4[:, :, 1, :]
                        cosb = cs.unsqueeze(1).broadcast_to([128, 8, 32])
                        sinb = sn.unsqueeze(1).broadcast_to([128, 8, 32])
                        t1v = t1[:].rearrange("p (h d) -> p h d", h=8)
                        t2v = t2[:].rearrange("p (h d) -> p h d", h=8)
                        P.op("pool", lambda g: g.tensor_tensor(t1v, x1, cosb, ALU.mult), reads=[bqrf], writes=[bt1])
                        P.op("pool", lambda g: g.tensor_tensor(t2v, x2, sinb, ALU.mult), reads=[bqrf], writes=[bt2])
                        P.op("pool", lambda g: g.tensor_tensor(o4[:, :, 0, :], t1v, t2v, ALU.subtract),
                             reads=[bt1, bt2], writes=[bqrb])
                        P.op("pool", lambda g: g.tensor_tensor(t1v, x2, cosb, ALU.mult), reads=[bqrf], writes=[bt1])
                        P.op("pool", lambda g: g.tensor_tensor(t2v, x1, sinb, ALU.mult), reads=[bqrf], writes=[bt2])
                        P.op("pool", lambda g: g.tensor_tensor(o4[:, :, 1, :], t1v, t2v, ALU.add),
                             reads=[bt1, bt2], writes=[bqrb])
                        for h in range(8):
                            P.op("pe", lambda g: g.transpose(psh(5)[0:64, h * 128:(h + 1) * 128], qrb[:, h * 64:(h + 1) * 64], ident[:]),
                                 reads=[bqrb], writes=[BPS[5]], sig=(h == 7))
                        P.op("act", lambda g: g.copy(qrT[:].rearrange("p (h t) -> p h t", h=8)[:, :, tok],
                                                     psh(5)[0:64, :].rearrange("p (h t) -> p h t", h=8)),
                             reads=[BPS[5]], writes=[bqrT])
                        for n in range(2):
                            pb = 6 + n
                            for kc in range(2):
                                P.op("pe", lambda g: g.matmul(
                                    psf(pb), ckvT[:, kc * T + blk * 128: kc * T + (blk + 1) * 128],
                                    wukv[:, kc * 1024 + n * 512: kc * 1024 + (n + 1) * 512], start=(kc == 0), stop=(kc == 1)),
                                    reads=[bckvT], writes=[BPS[pb]], sig=(kc == 1))
                            cast("dve" if n else "act", Vt[:, s * 1024 + n * 512: s * 1024 + (n + 1) * 512], psf(pb),
                                 [BPS[pb]], [bVt])
                    tk = slice(t * 512, (t + 1) * 512)
                    P.dma("sp", cqnT_d[:, :, tk].rearrange("c p t -> p c t"), cqT[:].rearrange("p (c t) -> p c t", c=3),
                          reads=[bcqT], writes=[bscr])
                    P.dma("sp", qrT_d[:, :, tk].rearrange("h p t -> p h t"), qrT[:].rearrange("p (h t) -> p h t", h=8),
                          reads=[bqrT], writes=[bscr])
                    P.dma("sp", v_d[tk, :].rearrange("(s p) f -> p s f", p=128), Vt[:].rearrange("p (s f) -> p s f", s=4),
                          reads=[bVt], writes=[bscr])
                P.barrier()
            with ExitStack() as esB:
                KTh = sbt(esB, [128, T], BF16)
                bKTh = Buf()
                Vh = sbt(esB, [128, T], BF16)
                bVh = Buf()
                cq = [sbt(esB, [128, 3 * 512], BF16) for _ in range(2)]
                bcq = [Buf(), Buf()]
                qr = [sbt(esB, [64, 512], BF16) for _ in range(2)]
                bqr = [Buf(), Buf()]
                QnT = sbt(esB, [128, 512], BF16)
                bQn = Buf()
                pT = [sbt(esB, [128, 512], BF16) for _ in range(6)]
                bpT = [Buf() for _ in range(6)]
                SBK = (3, 4, 5, 0, 1)
                rden = sbt(esB, [128, 512])
                brd = Buf()
                dacc = sbt(esB, [128, 512])
                bdacc = Buf()
                ones_f = sbt(esB, [128, 128])
                bof = Buf()
                P.op("pool", lambda g: g.memset(ones_f[:], 1.0), writes=[bof])
                oT = [sbt(esB, [128, 512], BF16) for _ in range(2)]
                boT = [Buf(), Buf()]
                batt = Buf()
                npt = 0
                nq = 0
                for h in range(8):
                    for c0 in range(0, NB, 16):
                        cn_ = min(16, NB - c0)
                        P.dma("sp", Vh[:, c0 * 128:(c0 + cn_) * 128].rearrange("p (c d) -> p c d", d=128),
                              v_d[c0 * 128:(c0 + cn_) * 128, h * 128:(h + 1) * 128].rearrange("(c p) d -> p c d", p=128),
                              reads=[bscr], writes=[bVh])
                    for tc in range(NT):
                        pb = tc % 2
                        for kc in range(2):
                            P.op("pe", lambda g: g.matmul(psf(pb), wukk[:, kc * 1024 + h * 128: kc * 1024 + (h + 1) * 128],
                                                          ckvT[:, kc * T + tc * 512: kc * T + (tc + 1) * 512],
                                                          start=(kc == 0), stop=(kc == 1)),
                                 reads=[bckvT], writes=[BPS[pb]], sig=(kc == 1))
                        cast("dve" if tc % 2 else "act", KTh[:, tc * 512:(tc + 1) * 512], psf(pb), [BPS[pb]], [bKTh])
                    for jq in range(NT):
                        qi = nq % 2
                        nq += 1
                        tk = slice(jq * 512, (jq + 1) * 512)
                        P.dma("sp", cq[qi][:].rearrange("p (c t) -> p c t", c=3), cqnT_d[:, :, tk].rearrange("c p t -> p c t"),
                              reads=[bscr], writes=[bcq[qi]])
                        P.dma("sp", qr[qi][:], qrT_d[h, :, tk], reads=[bscr], writes=[bqr[qi]])
                        for kc in range(3):
                            P.op("pe", lambda g: g.matmul(psf(2), wuqn[:, kc * 1024 + h * 128: kc * 1024 + (h + 1) * 128],
                                                          cq[qi][:, kc * 512:(kc + 1) * 512], start=(kc == 0), stop=(kc == 2)),
                                 reads=[bcq[qi]], writes=[BPS[2]], sig=(kc == 2))
                        cast("dve", QnT[:], psf(2), [BPS[2]], [bQn])
                        nch = 4 * (jq + 1)

                        def emit_S(c):
                            nonlocal npt
                            dg = c - 4 * jq
                            q0 = max(0, dg) * 128
                            pb = SBK[npt % 5]
                            pi = npt % 6
                            npt += 1
                            P.op("pe", lambda g: g.matmul(psf(pb)[:, q0:512], KTh[:, c * 128:(c + 1) * 128], QnT[:, q0:512],
                                                          start=True, stop=False),
                                 reads=[bKTh, bQn], writes=[BPS[pb]], sig=False)
                            P.op("pe", lambda g: g.matmul(psf(pb)[:, q0:512], krT[:, c * 128:(c + 1) * 128], qr[qi][:, q0:512],
                                                          start=False, stop=True),
                                 reads=[bkrT, bqr[qi]], writes=[BPS[pb]], sig=True)
                            P.op("act", lambda g: g.activation(pT[pi][:, q0:512], psf(pb)[:, q0:512], AF.Exp, scale=scale),
                                 reads=[BPS[pb]], writes=[bpT[pi]])
                            if dg >= 0:
                                P.op("pool", lambda g: g.tensor_tensor(pT[pi][:, q0:q0 + 128], pT[pi][:, q0:q0 + 128],
                                                                       mask[:, 0:128], ALU.mult),
                                     reads=[bpT[pi], bm], writes=[bpT[pi]])
                            return (c, pi, q0)

                        def emit_OD(c, pi, q0):
                            P.op("pe", lambda g: g.matmul(psf(6)[:, q0:512], Vh[:, c * 128:(c + 1) * 128], pT[pi][:, q0:512],
                                                          start=(c == 0), stop=(c == nch - 1)),
                                 reads=[bVh, bpT[pi]], writes=[BPS[6]], sig=(c == nch - 1))
                            P.op("pe", lambda g: g.matmul(psf(7)[:, q0:512], ones[:], pT[pi][:, q0:512],
                                                          start=(c == 0), stop=(c == nch - 1)),
                                 reads=[bpT[pi]], writes=[BPS[7]], sig=(c == nch - 1))

                        pend = []
                        for c in range(nch):
                            pend.append(emit_S(c))
                            if len(pend) > 3:
                                emit_OD(*pend.pop(0))
                        while pend:
                            emit_OD(*pend.pop(0))
                        P.op("act", lambda g: g.activation(rden[:], psf(7), AF.Ln), reads=[BPS[7]], writes=[brd])
                        P.op("act", lambda g: g.activation(rden[:], rden[:], AF.Exp, scale=-1.0), reads=[brd], writes=[brd])
                        P.op("dve", lambda g: g.tensor_tensor(oT[qi][:], psf(6), rden[:], ALU.mult),
                             reads=[BPS[6], brd], writes=[boT[qi]])
                        P.dma("sp", attT_d[h, :, tk], oT[qi][:], reads=[boT[qi]], writes=[batt])
                P.barrier()
            with ExitStack() as esC:
                ln = make_ln(esC, li, 0)
                aT = [sbt(esC, [128, 8 * 512], BF16) for _ in range(2)]
                baT = [Buf(), Buf()]
                xin = [sbt(esC, [128, 4 * 1024]) for _ in range(2)]
                bxin = [Buf(), Buf()]
                def loadC(t):
                    i = t % 2
                    tk = slice(t * 512, (t + 1) * 512)
                    P.dma("sp", aT[i][:].rearrange("p (h t) -> p h t", h=8), attT_d[:, :, tk].rearrange("h p t -> p h t"),
                          reads=[batt], writes=[baT[i]])
                    P.dma("sp", xin[i][:].rearrange("p (s f) -> p s f", s=4),
                          src[t * 512:(t + 1) * 512, :].rearrange("(s p) f -> p s f", p=128),
                          reads=[src_bufs[t * 4 + s] for s in range(4)] if src_bufs else [], writes=[bxin[i]])

                loadC(0)
                for t in range(NT):
                    i = t % 2
                    for s in range(4):
                        for n in range(2):
                            pb = (s % 2) * 2 + n
                            for hh in range(8):
                                P.op("pe", lambda g: g.matmul(psf(pb), aT[i][:, hh * 512 + s * 128: hh * 512 + (s + 1) * 128],
                                                              wo[:, hh * 1024 + n * 512: hh * 1024 + (n + 1) * 512],
                                                              start=(hh == 0), stop=(hh == 7)),
                                     reads=[baT[i]], writes=[BPS[pb]], sig=(hh == 7))
                        dst, bdst = dst_for(t * 4 + s, last)
                        ln_epilogue(ln, ((s % 2) * 2, (s % 2) * 2 + 1), xin[i][:, s * 1024:(s + 1) * 1024], bxin[i], dst, bdst)
                        if s == 0 and t + 1 < NT:
                            loadC(t + 1)
                ln_flush(ln)
                P.barrier()

    subs = []
    for li in range(DEPTH):
        subs.append((("swa", "lru", "mla")[li % 3], li))
        subs.append(("xattn", li))
        subs.append(("ffn", li))
    subs = subs[start:upto]
    for si, (kind, li) in enumerate(subs):
        first = si == 0
        last = si == len(subs) - 1
        src = x_in if first else xs_d
        sb_ = None if first else xs_b
        {"swa": phase_swa, "lru": phase_lru, "mla": phase_mla, "xattn": phase_xattn, "ffn": phase_ffn}[kind](li, src, sb_, last)
    P.barrier(["sp"])
    ges.close()
    return nc, P.nins


def make_consts(T):
    NB = T // 128
    k = np.arange(128)[:, None]
    q = np.arange(128)[None, :]
    mask = np.concatenate([(k <= q), (k > q)], axis=1).astype(np.float32)
    pos = (np.arange(NB)[None, :] * 128 + np.arange(128)[:, None]).astype(np.float32)
    iota = np.tile(np.arange(32, dtype=np.float32)[None, :], (128, 1))
    return {"c_ident": np.eye(128, dtype=np.float32), "c_mask": mask, "c_pos": pos, "c_iota": iota}


_CACHE = {}


def run(inputs, T, n_seq, upto=12, start=0):
    key = (T, upto, start)
    if key not in _CACHE:
        _CACHE[key] = build(T, upto, start)[0]
    nc = _CACHE[key]
    consts = make_consts(T)
    wts = {n: np.ascontiguousarray(np.asarray(inputs[n], dtype=np.float32)) for n, _ in WSHAPES}
    x = np.asarray(inputs["x"], dtype=np.float32)
    mem = np.asarray(inputs["mem"], dtype=np.float32)
    in_maps = []
    for c in range(N_CORES):
        b = c % n_seq
        m = {"x": np.ascontiguousarray(x[b]), "mem": np.ascontiguousarray(mem[b])}
        m.update(wts)
        m.update(consts)
        in_maps.append(m)
    res = run_bass_kernel_spmd(nc, in_maps, core_ids=list(range(N_CORES)))
    return np.stack([np.asarray(res.results[b]["out"]) for b in range(n_seq)], axis=0).astype(np.float32)


def kernel(**inputs):
    x = inputs["x"]
    B, S, _ = x.shape
    return run(inputs, S, B)
```
